# Optimizing a Trainium2 kernel written in Bass

```python
import math
import jax
import jax.numpy as jnp
from jax import lax
import numpy as np

D_MODEL = 1024
BATCH = 8
SEQ = 4096
DEPTH = 2

N_MEM = 256
H_A = 8
KVH_A = 2
G_A = H_A // KVH_A
DH_A = 64
WINDOW = 128
BLOCK = 128
KEY_SPAN = BLOCK + 2 * WINDOW
N_BUCKETS = 32
MAX_DISTANCE = 128
H_B = 4
DK_B = 64
DV_B = 128
GLA_RANK = 16
GLA_CHUNK = 64
GLA_NORMALIZER = 16.0
H_M = 4
DH_M = 128
BRANCH_W = 512
N_BRANCH = 3
D_FF = -(-8 * D_MODEL // (3 * 256)) * 256
EPS = 1e-6
NEG_INF = -1e30
IN_SIZES = (H_A * DH_A, KVH_A * DH_A, KVH_A * DH_A, H_B * DK_B, H_B * DK_B, H_B * DV_B, H_B * DV_B, 2 * GLA_RANK, H_M * DH_M, N_BRANCH * D_MODEL)
D_IN = sum(IN_SIZES)

kernel_name = "hybrid_gated_parallel_encoder"


def rmsnorm(x, g):
    xf = x.astype(jnp.float32)
    y = xf * lax.rsqrt(jnp.mean(xf * xf, axis=-1, keepdims=True) + EPS)
    return (y * g.astype(jnp.float32)).astype(x.dtype)


def t5_bucket(rel):
    nb = N_BUCKETS // 2
    max_exact = nb // 2
    ret = (rel > 0).astype(jnp.int32) * nb
    n = jnp.abs(rel)
    nf = jnp.maximum(n, 1).astype(jnp.float32)
    large = max_exact + (jnp.log(nf / max_exact) / math.log(MAX_DISTANCE / max_exact) * (nb - max_exact)).astype(jnp.int32)
    large = jnp.minimum(large, nb - 1)
    return ret + jnp.where(n < max_exact, n, large)


def window_attention(q, k, v, bias, sink):
    Bn, S = q.shape[0], q.shape[1]
    nblk = S // BLOCK
    pad = ((0, 0), (WINDOW, WINDOW), (0, 0), (0, 0))
    k_pad = jnp.pad(k, pad)
    v_pad = jnp.pad(v, pad)
    q_blocks = q.reshape(Bn, nblk, BLOCK, KVH_A, G_A, DH_A).swapaxes(0, 1)
    t = jnp.arange(BLOCK)[:, None]
    j = jnp.arange(KEY_SPAN)[None, :]
    in_band = jnp.abs(j - WINDOW - t) <= WINDOW
    scale = DH_A ** -0.5
    sink_f = sink.astype(jnp.float32)[:, :, None]

    def one_block(args):
        qb, i = args
        start = i * BLOCK
        kb = lax.dynamic_slice_in_dim(k_pad, start, KEY_SPAN, axis=1)
        vb = lax.dynamic_slice_in_dim(v_pad, start, KEY_SPAN, axis=1)
        kpos = start - WINDOW + jnp.arange(KEY_SPAN)
        valid = in_band & ((kpos >= 0) & (kpos < S))[None, :]
        s = jnp.einsum('bqhgd,bkhd->bhgqk', qb, kb).astype(jnp.float32) * scale + bias
        s = jnp.where(valid, s, NEG_INF)
        m = jnp.maximum(jnp.max(s, axis=-1), sink_f)
        p = jnp.exp(s - m[..., None])
        denom = jnp.sum(p, axis=-1) + jnp.exp(sink_f - m)
        o = jnp.einsum('bhgqk,bkhd->bqhgd', p.astype(v.dtype), vb)
        return o / denom.transpose(0, 3, 1, 2)[..., None].astype(o.dtype)

    out = lax.map(one_block, (q_blocks, jnp.arange(nblk)))
    return out.swapaxes(0, 1).reshape(Bn, S, H_A * DH_A)


def gla_chunked(q, k, v, log_a, strict):
    Bn, S, H, DK = q.shape
    DV = v.shape[-1]
    N = S // GLA_CHUNK

    def to_chunks(t):
        return t.astype(jnp.float32).reshape(Bn, N, GLA_CHUNK, H, t.shape[-1]).transpose(1, 0, 3, 2, 4)

    qc, kc, vc, gc = to_chunks(q), to_chunks(k), to_chunks(v), to_chunks(log_a)
    b = jnp.cumsum(gc, axis=3)
    b_last = b[:, :, :, -1:, :]
    qd = qc * jnp.exp(b)
    kd = kc * jnp.exp(-b)
    k_end = kc * jnp.exp(b_last - b)
    mask = jnp.tril(jnp.ones((GLA_CHUNK, GLA_CHUNK), dtype=bool), k=-1 if strict else 0)
    attn = jnp.where(mask, jnp.einsum('nbhcd,nbhsd->nbhcs', qd, kd), 0.0)
    o_intra = jnp.einsum('nbhcs,nbhse->nbhce', attn, vc)
    u = jnp.einsum('nbhsd,nbhse->nbhde', k_end, vc)
    decay = jnp.exp(b_last[:, :, :, 0, :])

    def step(state, inp):
        d, u_n = inp
        return state * d[..., None] + u_n, state

    _, s_prev = lax.scan(step, jnp.zeros((Bn, H, DK, DV), jnp.float32), (decay, u))
    o_inter = jnp.einsum('nbhcd,nbhde->nbhce', qd, s_prev)
    o = (o_intra + o_inter).transpose(1, 0, 3, 2, 4).reshape(Bn, S, H, DV)
    return o.astype(v.dtype)


def gla_branch(q, k, v, og, lr, w_up, b_dec, g_norm):
    Bn, S = q.shape[0], q.shape[1]
    q = q.reshape(Bn, S, H_B, DK_B) * (DK_B ** -0.5)
    k = k.reshape(Bn, S, H_B, DK_B)
    v = v.reshape(Bn, S, H_B, DV_B)
    logits = jnp.einsum('bsjr,jrk->bsjk', lr.reshape(Bn, S, 2, GLA_RANK).astype(jnp.float32), w_up.astype(jnp.float32)) + b_dec.astype(jnp.float32)
    la = (jax.nn.log_sigmoid(logits) / GLA_NORMALIZER).reshape(Bn, S, 2, H_B, DK_B)
    o_fwd = gla_chunked(q, k, v, la[:, :, 0], strict=False)
    flip = lambda t: jnp.flip(t, axis=1)
    o_bwd = flip(gla_chunked(flip(q), flip(k), flip(v), flip(la[:, :, 1]), strict=True))
    o = rmsnorm(o_fwd + o_bwd, g_norm).reshape(Bn, S, H_B * DV_B)
    return o * jax.nn.silu(og)


def memory_attention(q, k, v):
    Bn, S = q.shape[0], q.shape[1]
    s = jnp.einsum('bshd,bmhd->bhsm', q, k).astype(jnp.float32) * (DH_M ** -0.5)
    p = jax.nn.softmax(s, axis=-1).astype(v.dtype)
    return jnp.einsum('bhsm,bmhd->bshd', p, v).reshape(Bn, S, H_M * DH_M)


def setup_inputs(seed: int = 0) -> dict:
    key = jax.random.key(seed)
    ks = jax.random.split(key, 20)
    f32 = jnp.float32

    def nrm(k, shape, scale):
        return jax.random.normal(k, shape, f32) * scale

    def gain(k, shape):
        return 1.0 + 0.02 * jax.random.normal(k, shape, f32)

    return {
        'x': nrm(ks[0], (BATCH, SEQ, D_MODEL), 1.0),
        'mem': nrm(ks[1], (BATCH, N_MEM, D_MODEL), 1.0),
        'rel_bias': nrm(ks[2], (N_BUCKETS, H_A), 0.5),
        'norm_mix_g': gain(ks[3], (DEPTH, D_MODEL)),
        'norm_ffn_g': gain(ks[4], (DEPTH, D_MODEL)),
        'norm_mem_g': gain(ks[5], (DEPTH, D_MODEL)),
        'w_in': nrm(ks[6], (DEPTH, D_MODEL, D_IN), D_MODEL ** -0.5),
        'q_norm_a': gain(ks[7], (DEPTH, DH_A)),
        'k_norm_a': gain(ks[8], (DEPTH, DH_A)),
        'sink_a': nrm(ks[9], (DEPTH, H_A), 0.5),
        'w_decay_up': nrm(ks[10], (DEPTH, 2, GLA_RANK, H_B * DK_B), GLA_RANK ** -0.5),
        'b_decay': nrm(ks[11], (DEPTH, 2, H_B * DK_B), 0.1),
        'gla_norm_g': gain(ks[12], (DEPTH, DV_B)),
        'w_mem_kv': nrm(ks[13], (DEPTH, D_MODEL, 2 * H_M * DH_M), D_MODEL ** -0.5),
        'q_norm_m': gain(ks[14], (DEPTH, DH_M)),
        'k_norm_m': gain(ks[15], (DEPTH, DH_M)),
        'w_branch': nrm(ks[16], (DEPTH, N_BRANCH, BRANCH_W, D_MODEL), BRANCH_W ** -0.5),
        'w_out': nrm(ks[17], (DEPTH, D_MODEL, D_MODEL), 0.5 * D_MODEL ** -0.5),
        'w_ffn_in': nrm(ks[18], (DEPTH, D_MODEL, 2 * D_FF), D_MODEL ** -0.5),
        'w_ffn_out': nrm(ks[19], (DEPTH, D_FF, D_MODEL), 0.5 * D_FF ** -0.5),
    }


def reference(x, mem, rel_bias, norm_mix_g, norm_ffn_g, norm_mem_g, w_in, q_norm_a, k_norm_a, sink_a, w_decay_up, b_decay, gla_norm_g, w_mem_kv, q_norm_m, k_norm_m, w_branch, w_out, w_ffn_in, w_ffn_out):
    Bn, S, _ = x.shape
    n_mem = mem.shape[1]
    t = jnp.arange(BLOCK)[:, None]
    j = jnp.arange(KEY_SPAN)[None, :]
    rel = j - WINDOW - t
    band_bias = rel_bias[t5_bucket(rel)].astype(jnp.float32)
    band_bias = band_bias.transpose(2, 0, 1).reshape(KVH_A, G_A, BLOCK, KEY_SPAN)
    split_at = np.cumsum(IN_SIZES)[:-1].tolist()

    for l in range(DEPTH):
        h = rmsnorm(x, norm_mix_g[l])
        proj = h @ w_in[l]
        qa, ka, va, qb, kb, vb, gb, lr, qm, gl = jnp.split(proj, split_at, axis=-1)

        qa = rmsnorm(qa.reshape(Bn, S, KVH_A, G_A, DH_A), q_norm_a[l])
        ka = rmsnorm(ka.reshape(Bn, S, KVH_A, DH_A), k_norm_a[l])
        va = va.reshape(Bn, S, KVH_A, DH_A)
        o_a = window_attention(qa, ka, va, band_bias, sink_a[l].reshape(KVH_A, G_A))

        o_b = gla_branch(qb, kb, vb, gb, lr, w_decay_up[l], b_decay[l], gla_norm_g[l])

        mn = rmsnorm(mem, norm_mem_g[l])
        km, vm = jnp.split(mn @ w_mem_kv[l], 2, axis=-1)
        qm = rmsnorm(qm.reshape(Bn, S, H_M, DH_M), q_norm_m[l])
        km = rmsnorm(km.reshape(Bn, n_mem, H_M, DH_M), k_norm_m[l])
        o_m = memory_attention(qm, km, vm.reshape(Bn, n_mem, H_M, DH_M))

        branches = jnp.stack([o_a, o_b, o_m], axis=2)
        gates = jax.nn.sigmoid(gl.reshape(Bn, S, N_BRANCH, D_MODEL))
        merged = jnp.sum(gates * jnp.einsum('bsjc,jcd->bsjd', branches, w_branch[l]), axis=2)
        x = x + merged @ w_out[l]

        h2 = rmsnorm(x, norm_ffn_g[l])
        gate, up = jnp.split(h2 @ w_ffn_in[l], 2, axis=-1)
        x = x + (jax.nn.silu(gate) * up) @ w_ffn_out[l]
    return x
```

```python
import math
from contextlib import ExitStack
import numpy as np
import concourse.bass as bass
import concourse.mybir as mybir
from concourse.bass_utils import run_bass_kernel_spmd

F32 = mybir.dt.float32
BF16 = mybir.dt.bfloat16
ALU = mybir.AluOpType
AF = mybir.ActivationFunctionType

D = 1024
D_IN = 5920
D_FF = 2816
N_MEM = 256
EPS = 1e-6
SEM_LIMIT = 30000
MARKS = []
NLANES = 12
C_A = 8.0
C_M = math.sqrt(128.0)
O_QA, O_KA, O_VA, O_QB, O_KB, O_VB, O_GB, O_LR, O_QM, O_GL = 0, 512, 640, 768, 1024, 1280, 1792, 2304, 2336, 2848


class Buf:
    def __init__(self, name):
        self.name = name
        self.w = []
        self.r = {}


class T:
    def __init__(self, h, name):
        self.h = h
        self.b = Buf(name)

    def __getitem__(self, k):
        return self.h[k]


class Eng:
    def __init__(self, kern, name, h):
        self.k = kern
        self.name = name
        self.h = h
        self.seen = {}
        self.nsem = 0
        self.newsem()
        self.lanes = None
        self.li = 0

    def newsem(self):
        self.sem = self.k.new_sem(f"e_{self.name}_{self.nsem}")
        self.nsem += 1
        self.cnt = 0

    def wait(self, tok):
        sem, val, _ = tok
        if self.seen.get(sem, 0) >= val:
            return
        self.h.wait_ge(sem, val)
        self.seen[sem] = val


class Kern:
    def __init__(self, nc, es):
        self.nc = nc
        self.es = es
        self.nsems = 0
        self.pe = Eng(self, "pe", nc.tensor)
        self.act = Eng(self, "act", nc.scalar)
        self.dve = Eng(self, "dve", nc.vector)
        self.pool = Eng(self, "pool", nc.gpsimd)
        self.sp = Eng(self, "sp", nc.sync)
        for e in (self.sp, self.pool):
            e.lanes = [[self.new_sem(f"l_{e.name}_{i}"), 0] for i in range(NLANES)]

    def new_sem(self, name):
        self.nsems += 1
        return self.es.enter_context(self.nc.semaphore(f"{name}_{self.nsems}"))

    def _deps(self, eng, reads, writes, join=False):
        for t in reads:
            for tok in t.b.w:
                eng.wait(tok)
        for t in writes:
            b = t.b
            for tok in b.w:
                if eng.name == "pe" and tok[2] == "pe":
                    continue
                if join and tok[2].startswith("dma:"):
                    continue
                eng.wait(tok)
            for tok in b.r.values():
                if not (tok[2] == eng.name and eng.name == "pe"):
                    eng.wait(tok)

    def _update(self, key, tok, reads, writes, join=False):
        for t in writes:
            if join:
                t.b.w = [w_ for w_ in t.b.w if w_[2].startswith("dma:")] + [tok]
            else:
                t.b.w = [tok]
            t.b.r = {}
        for t in reads:
            if tok not in t.b.w:
                t.b.r[key] = tok

    def op(self, eng, fn, reads=(), writes=()):
        self._deps(eng, reads, writes)
        ins = fn()
        eng.cnt += 1
        eng.total = getattr(eng, 'total', 0) + 1
        ins.then_inc(eng.sem, 1)
        tok = (eng.sem, eng.cnt, eng.name)
        self._update(eng.name, tok, reads, writes)
        if eng.cnt >= SEM_LIMIT:
            eng.newsem()
        return tok

    def dma(self, eng, out, in_, reads=(), writes=(), join=False):
        self._deps(eng, reads, writes, join)
        li = eng.li % NLANES
        eng.li += 1
        lane = eng.lanes[li]
        if lane[1] >= SEM_LIMIT:
            eng.wait((lane[0], lane[1], "x"))
            lane[0] = self.new_sem(f"l_{eng.name}_{li}")
            lane[1] = 0
        if lane[1] > 0:
            eng.wait((lane[0], lane[1], "x"))
        ins = eng.h.dma_start(out=out, in_=in_)
        lane[1] += 16
        ins.then_inc(lane[0], 16)
        key = f"dma:{eng.name}:{li}"
        tok = (lane[0], lane[1], key)
        self._update(key, tok, reads, writes, join)
        return tok

    def finish(self):
        for e in (self.sp, self.pool):
            for lane in e.lanes:
                if lane[1] > 0:
                    e.wait((lane[0], lane[1], "x"))


def build(S, depth=2, TG=512):
    NT = S // 128
    G = S // TG
    NTG = TG // 128
    nc = bass.Bass("TRN2", target_bir_lowering=False)
    es = ExitStack()
    es.enter_context(nc.allow_low_precision("bf16 matmul operands, fp32 accumulation"))
    es.enter_context(nc.allow_non_contiguous_dma("small strided param loads"))
    K = Kern(nc, es)
    pe, act, dve, pool, sp = K.pe, K.act, K.dve, K.pool, K.sp

    def din(name, shape, dt=F32):
        return T(nc.dram_tensor(name, list(shape), dt, kind="ExternalInput").ap(), name)

    def dscr(name, shape, dt):
        return T(nc.dram_tensor(name, list(shape), dt).ap(), name)

    x_in = din("x", [S, D])
    mem_in = din("mem", [N_MEM, D])
    biasg = din("biasg", [128, 6, 512])
    p_nmix = din("norm_mix_g", [2, D]); p_nffn = din("norm_ffn_g", [2, D]); p_nmem = din("norm_mem_g", [2, D])
    w_in = din("w_in", [2, D, D_IN])
    p_qna = din("q_norm_a", [2, 64]); p_kna = din("k_norm_a", [2, 64]); p_sink = din("sink_a", [2, 8])
    p_wup = din("w_decay_up", [2, 2, 16, 256]); p_bdec = din("b_decay", [2, 2, 256]); p_gng = din("gla_norm_g", [2, 128])
    w_kv = din("w_mem_kv", [2, D, D])
    p_qnm = din("q_norm_m", [2, 128]); p_knm = din("k_norm_m", [2, 128])
    w_br = din("w_branch", [2, 3, 512, D]); w_out = din("w_out", [2, D, D])
    w_fi = din("w_ffn_in", [2, D, 2 * D_FF]); w_fo = din("w_ffn_out", [2, D_FF, D])
    y_out = T(nc.dram_tensor("y", [S, D], F32, kind="ExternalOutput").ap(), "y")

    T_QA, T_QBKB, T_VB, T_GB, T_QM, T_P1, T_KV0, T_KV1, T_OUT0, T_OUT1, T_MERGE, T_FFI, NT_W = 0, 1, 2, 3, 4, 5, 6, 7, 8, 9, 10, 18, 29
    WT_h = nc.dram_tensor("WT", [2, NT_W, 128, 8, 512], BF16).ap()
    WFO_h = nc.dram_tensor("WFO", [2, 4, 128, 11, 512], BF16).ap()
    WMA_h = nc.dram_tensor("WMA", [2, 8, 128, 4, 128], BF16).ap()
    WL_h = nc.dram_tensor("WL", [2, 128, 8, 32], BF16).ap()
    wt = [[T(WT_h[l, t], f"WT{l}_{t}") for t in range(NT_W)] for l in range(2)]
    wfod = [[T(WFO_h[l, i], f"WFO{l}_{i}") for i in range(4)] for l in range(2)]
    wmad = [[T(WMA_h[l, c], f"WMA{l}_{c}") for c in range(8)] for l in range(2)]
    wld = [T(WL_h[l], f"WL{l}") for l in range(2)]
    xbuf = dscr("xbuf", [S, D], F32)
    hTd = dscr("hTd", [128, 8, S], BF16)
    kaTd = dscr("kaTd", [64, 2, S], BF16)
    vad = dscr("vad", [128, NT, 128], BF16)
    Sbd = dscr("Sbd", [64, NT, 512], BF16)

    def sb(name, shape, dt):
        return T(es.enter_context(nc.sbuf_tensor(name, list(shape), dt)), name)

    def psum(name, shape, dt):
        return T(es.enter_context(nc.psum_tensor(name, list(shape), dt)), name)

    PS = [psum(f"ps{i}", [128, 512], F32) for i in range(7)]
    PST = psum("pst", [128, 1024], BF16)
    psi = [0]

    def ps():
        p = PS[psi[0] % len(PS)]
        psi[0] += 1
        return p

    def mm(out, lhsT, rhs, start, stop, reads, writes):
        K.op(pe, lambda: nc.tensor.matmul(out, lhsT, rhs, start=start, stop=stop), reads, writes)

    ident = sb("ident", [128, 128], BF16)
    ones_bf = sb("ones_bf", [128, 128], BF16)
    ones32 = sb("ones32", [128, 128], F32)
    tril = sb("tril", [128, 128], F32)
    triu = sb("triu", [128, 128], F32)
    mask_f = sb("mask_f", [128, 128], F32)
    mask_b = sb("mask_b", [128, 128], F32)
    biasT = sb("biasT", [128, 6, 512], F32)
    K.op(pool, lambda: nc.gpsimd.memset(ones32[:], 1.0), [], [ones32])
    K.op(pool, lambda: nc.gpsimd.memset(ones_bf[:], 1.0), [], [ones_bf])
    K.op(pool, lambda: nc.gpsimd.affine_select(ident[:], ones_bf[:], [[-1, 128]], ALU.is_equal, 0.0, base=0, channel_multiplier=1), [ones_bf], [ident])
    K.op(pool, lambda: nc.gpsimd.affine_select(tril[:], ones32[:], [[1, 128]], ALU.is_ge, 0.0, base=0, channel_multiplier=-1), [ones32], [tril])
    K.op(pool, lambda: nc.gpsimd.affine_select(triu[:], ones32[:], [[-1, 128]], ALU.is_ge, 0.0, base=0, channel_multiplier=1), [ones32], [triu])
    K.op(pool, lambda: nc.gpsimd.memset(mask_f[:], 1.0), [], [mask_f])
    K.op(pool, lambda: nc.gpsimd.memset(mask_b[:], 1.0), [], [mask_b])
    K.op(pool, lambda: nc.gpsimd.affine_select(mask_f[:], mask_f[:], [[1, 128]], ALU.is_ge, 0.0, base=0, channel_multiplier=-1), [mask_f], [mask_f])
    K.op(pool, lambda: nc.gpsimd.affine_select(mask_b[:], mask_b[:], [[-1, 128]], ALU.is_ge, 0.0, base=-1, channel_multiplier=1), [mask_b], [mask_b])
    K.dma(sp, biasT[:], biasg[:], [biasg], [biasT])
    for kvh in range(2):
        v0 = biasT[:, kvh * 3 + 0, :].rearrange("p (h c) -> p h c", h=4)
        K.op(pool, lambda v0=v0: nc.gpsimd.affine_select(v0, v0, [[0, 4], [-1, 128]], ALU.is_ge, -1e30, base=0, channel_multiplier=1), [biasT], [biasT])
        v2 = biasT[:, kvh * 3 + 2, :].rearrange("p (h c) -> p h c", h=4)
        K.op(pool, lambda v2=v2: nc.gpsimd.affine_select(v2, v2, [[0, 4], [1, 128]], ALU.is_ge, -1e30, base=0, channel_multiplier=-1), [biasT], [biasT])

    def cast_jobs(l):
        J = []

        def rows(src, c0, n):
            return src[l, :, c0:c0 + n].rearrange("(k p) c -> p k c", p=128)

        def add(t, d0, src, c0, n):
            J.append((wt[l][t], wt[l][t][:, :, d0:d0 + n], src, rows(src, c0, n)))
        add(T_KV0, 0, w_kv, 0, 512); add(T_KV1, 0, w_kv, 512, 512)
        add(T_P1, 0, w_in, O_KA, 256); add(T_P1, 256, w_in, O_KB, 256)
        add(T_VB, 0, w_in, O_VB, 512)
        J.append((wld[l], wld[l][:, :, :], w_in, rows(w_in, O_LR, 32)))
        add(T_QA, 0, w_in, O_QA, 512); add(T_QBKB, 0, w_in, O_QB, 512); add(T_GB, 0, w_in, O_GB, 512); add(T_QM, 0, w_in, O_QM, 512)
        for c in range(8):
            cs = slice(c * 128, (c + 1) * 128)
            for kvh in range(2):
                J.append((wmad[l][c], wmad[l][c][kvh * 64:(kvh + 1) * 64, :, :], w_br,
                          w_br[l, 0, kvh * 256:(kvh + 1) * 256, cs].rearrange("(h p) c -> p h c", p=64)))
            for j in range(3):
                add(T_MERGE + c, j * 128, w_in, O_GL + j * D + c * 128, 128)
            for j in (1, 2):
                J.append((wt[l][T_MERGE + c], wt[l][T_MERGE + c][:, 4 * (j - 1):4 * j, 384:512], w_br,
                          w_br[l, j, :, cs].rearrange("(h p) c -> p h c", p=128)))
        add(T_OUT0, 0, w_out, 0, 512); add(T_OUT1, 0, w_out, 512, 512)
        for fc in range(11):
            for u in range(2):
                add(T_FFI + fc, u * 256, w_fi, u * D_FF + fc * 256, 256)
        for half in range(2):
            for kh in range(2):
                J.append((wfod[l][half * 2 + kh], wfod[l][half * 2 + kh][:, :, :], w_fo,
                          w_fo[l, kh * 1408:(kh + 1) * 1408, half * 512:(half + 1) * 512].rearrange("(k p) c -> p k c", p=128)))
        return J

    def issue_casts(jobs, n):
        for _ in range(min(n, len(jobs))):
            dst_t, dst_ap, src_t, src_ap = jobs.pop(0)
            K.dma(pool, dst_ap, src_ap, [src_t], [dst_t], join=True)

    wup = sb("wup", [16, 2, 256], BF16)
    jobs0 = cast_jobs(0)
    K.dma(pool, wup[:], p_wup[0].rearrange("d r c -> r d c"), [p_wup], [wup])
    issue_casts(jobs0, len(jobs0))
    jobs1 = cast_jobs(1) if depth > 1 else []
    per_group = -(-len(jobs1) // max(1, G - 1)) if G > 1 else len(jobs1)

    gmix = sb("gmix", [128, D], F32); gffn = sb("gffn", [128, D], F32); gmem = gffn
    bdec = sb("bdec", [128, 512], F32)
    qna = sb("qna", [64, 1], F32); kna = sb("kna", [64, 1], F32)
    qnm = sb("qnm", [128, 1], F32); knm = sb("knm", [128, 1], F32); gng = sb("gng", [128, 1], F32)
    sinkraw = sb("sinkraw", [64, 8], F32); sinkexp = sb("sinkexp", [64, 8], F32); sinkrow = sb("sinkrow", [1, 2, 512], BF16)
    kmT = sb("kmT", [128, 4, 256], BF16); vm = sb("vm", [128, 2, 512], BF16)
    xg = sb("xg", [128, NTG, D], F32)
    hT = sb("hT", [128, 8, TG], BF16)
    hbfs = [sb(f"hbf{i}", [128, D], BF16) for i in range(4)]
    st1s = [sb(f"st1_{i}", [128, 1], F32) for i in range(4)]
    st2s = [sb(f"st2_{i}", [128, 1], F32) for i in range(4)]
    wA = [sb(f"wA{i}", [128, 8, 512], BF16) for i in range(3)]
    wai = [0]
    wl = sb("wl", [128, 8, 32], BF16)
    def view(ap, name):
        return T(ap, name)

    assert TG == 512
    arena1 = es.enter_context(nc.sbuf_tensor("arena1", [128, 11264], BF16))
    actT = view(arena1[:, :].rearrange("p (f c) -> p f c", f=22), "actT")
    qaT = view(arena1[0:64, 0:4096].rearrange("p (h c) -> p h c", h=8), "qaT")
    qbT32 = view(arena1[0:64, 4096:6144].rearrange("p (h c) -> p h c", h=4), "qbT")
    kbT32 = view(arena1[0:64, 6144:8192].rearrange("p (h c) -> p h c", h=4), "kbT")
    sog = view(arena1[:, 8192:10240].rearrange("p (h c) -> p h c", h=4), "sog")
    lrT = view(arena1[0:16, 10240:11264].rearrange("p (d c) -> p d c", d=2), "lrT")
    A1V = [qaT, qbT32, kbT32, sog, lrT]
    arena2 = es.enter_context(nc.sbuf_tensor("arena2", [128, 6144], BF16))
    arena3 = es.enter_context(nc.sbuf_tensor("arena3", [128, 6144], BF16))
    wfo = [view(arena2[:, 0:5632].rearrange("p (f c) -> p f c", f=11), "wfo0"),
           view(arena3[:, 0:5632].rearrange("p (f c) -> p f c", f=11), "wfo1")]
    oaT = view(arena2[:, 0:2048].rearrange("p (h c) -> p h c", h=4), "oaT")
    omT = view(arena2[:, 4096:6144].rearrange("p (h c) -> p h c", h=4), "omT")
    mergedT = view(arena3[:, 0:4096].rearrange("p (k c) -> p k c", k=8), "mergedT")
    obT = view(arena3[:, 4096:6144].rearrange("p (h c) -> p h c", h=4), "obT")

    def handoff(srcs, dsts):
        for d_ in dsts:
            for s_ in srcs:
                if s_ is d_:
                    continue
                for i_, w_ in enumerate(s_.b.w):
                    d_.b.r["h:" + s_.b.name + ":w%d" % i_] = w_
                for k_, tok_ in list(s_.b.r.items()):
                    d_.b.r["h:" + s_.b.name + ":" + k_] = tok_

    kbtm = sb("kbtm", [128, NTG, 256], F32)
    vbtm = sb("vbtm", [128, NTG, 512], BF16)
    qmT = sb("qmT", [128, 4, TG], BF16)
    kaT = sb("kaT", [64, 2, TG + 256], BF16); va = sb("va", [128, NTG + 2, 128], BF16)
    obT32 = sb("obT32", [128, 4, TG], F32)
    f32a = sb("f32a", [128, 512], F32); f32b = sb("f32b", [128, 512], F32); f32c = sb("f32c", [128, 512], F32)
    f32d = sb("f32d", [128, 512], F32)
    sqb = sb("sqb", [128, 512], BF16)
    sqb2 = sb("sqb2", [128, 512], BF16)
    pT = [sb(f"pT{i}", [128, 512], BF16) for i in range(3)]
    nlas = [sb(f"nla{i}", [128, 512], F32) for i in range(2)]
    Eq = [sb(f"Eq{d}", [64, 512], F32) for d in range(2)]
    Ek0 = sb("Ek", [64, 512], F32)
    Ek = [Ek0, Ek0]
    qds = [[sb(f"qd{i}_{d}", [64, 512], BF16) for d in range(2)] for i in range(2)]
    kds = [[sb(f"kd{i}_{d}", [64, 512], BF16) for d in range(2)] for i in range(2)]
    Etm = sb("Etm", [128, 256], F32)
    kdtms = [sb(f"kdtm{i}", [128, 256], BF16) for i in range(2)]
    Ams = [[sb(f"Am{i}_{d}", [128, 512], BF16) for d in range(2)] for i in range(2)]
    decs = [sb(f"dec{i}", [64, 4], F32) for i in range(2)]
    Rst = sb("Rst", [64, 512], F32); Sf = sb("Sf", [64, 512], F32)
    Sfbs = [sb(f"Sfb{i}", [64, 512], BF16) for i in range(2)]
    dprev = sb("dprev", [64, 4], F32)
    Sblc = [sb(f"Sblc{i}", [64, 512], BF16) for i in range(2)]
    wbA = [sb(f"wbA{i}", [128, 4, 128], BF16) for i in range(2)]
    mnT = hT
    print("SBUF bytes remaining per partition:", nc.sbuf_bytes_remaining)

    def wnext():
        w = wA[wai[0] % 3]
        wai[0] += 1
        return w

    def wview(wsc, l, c0, ncols, rows=D):
        return wsc[l, 0:rows, c0:c0 + ncols].rearrange("(k p) c -> p k c", p=128)

    def rsqrt_col(dst, ss, n, parts=128):
        K.op(act, lambda: nc.scalar.activation(dst[0:parts, :], ss[0:parts, :], AF.Sqrt, bias=EPS, scale=1.0 / n), [ss], [dst])
        K.op(dve, lambda: nc.vector.reciprocal(dst[0:parts, :], dst[0:parts, :]), [dst], [dst])

    def norm_chain(src_t, src_ap, gt, i):
        hb, s1_, s2_ = hbfs[i], st1s[i], st2s[i]
        K.op(act, lambda: nc.scalar.activation(hb[:], src_ap, AF.Square, accum_out=s1_[:]), [src_t], [hb, s1_])
        rsqrt_col(s2_, s1_, D)
        K.op(dve, lambda: nc.vector.scalar_tensor_tensor(hb[:], src_ap, s2_[:], gt[:], ALU.mult, ALU.mult), [src_t, s2_, gt], [hb])

    def norm_tr(i, dstT, col0):
        hb = hbfs[i]
        for k in range(8):
            K.op(pe, lambda k=k: nc.tensor.transpose(PST[:, k * 128:(k + 1) * 128], hb[:, k * 128:(k + 1) * 128], ident[:]), [hb, ident], [PST])
        K.op(act, lambda: nc.scalar.copy(dstT[:, :, col0:col0 + 128], PST[:].rearrange("p (k c) -> p k c", k=8)), [PST], [dstT])

    def norm_transpose(src_t, src_ap, gt, dstT, col0, i=0):
        norm_chain(src_t, src_ap, gt, i)
        norm_tr(i, dstT, col0)

    def qknorm_p1(p, parts, ntok):
        K.op(act, lambda: nc.scalar.activation(sqb[0:parts, 0:ntok], p[0:parts, 0:ntok], AF.Square), [p], [sqb])
        K.op(act, lambda: nc.scalar.copy(f32b[0:parts, 0:ntok], p[0:parts, 0:ntok]), [p], [f32b])

    def qknorm_p2(parts, ntok, gcol, dst_ap, dst_t, n):
        p2 = ps()
        mm(p2[0:parts, 0:ntok], ones_bf[0:parts, 0:parts], sqb[0:parts, 0:ntok], True, True, [ones_bf, sqb], [p2])
        K.op(act, lambda: nc.scalar.activation(f32c[0:parts, 0:ntok], p2[0:parts, 0:ntok], AF.Ln, bias=EPS, scale=1.0 / n), [p2], [f32c])
        K.op(act, lambda: nc.scalar.activation(f32c[0:parts, 0:ntok], f32c[0:parts, 0:ntok], AF.Exp, scale=-0.5), [f32c], [f32c])
        K.op(dve, lambda: nc.vector.scalar_tensor_tensor(dst_ap, f32b[0:parts, 0:ntok], gcol[0:parts, :], f32c[0:parts, 0:ntok], ALU.mult, ALU.mult), [f32b, gcol, f32c], [dst_t])

    def qknorm_fm(p, parts, ntok, gcol, dst_ap, dst_t, n):
        qknorm_p1(p, parts, ntok)
        qknorm_p2(parts, ntok, gcol, dst_ap, dst_t, n)

    def proj_norm_pipelined(specs):
        pend = None
        for (pf, parts, ntok, gcol, dst_ap, dst_t, n) in specs:
            p = pf()
            if pend is not None:
                qknorm_p2(*pend)
            qknorm_p1(p, parts, ntok)
            pend = (parts, ntok, gcol, dst_ap, dst_t, n)
        if pend is not None:
            qknorm_p2(*pend)

    def proj_fm(w, c0, M, rhsT, ntok, nk=8):
        p = ps()
        for k in range(nk):
            mm(p[0:M, 0:ntok], w[:, k, c0:c0 + M], rhsT[:, k, 0:ntok], k == 0, k == nk - 1, [w, rhsT], [p])
        return p

    def proj_tm(w, c0, N, lhsT, t0, nk=8):
        p = ps()
        for k in range(nk):
            mm(p[:, 0:N], lhsT[:, k, t0:t0 + 128], w[:, k, c0:c0 + N], k == 0, k == nk - 1, [w, lhsT], [p])
        return p

    f32d_h = [T(f32d.h[:, 0:256], "f32d_lo"), T(f32d.h[:, 256:512], "f32d_hi")]

    def gla_logits(t, d, lrrow, nla, bank=None):
        tmp = f32d_h[d]
        p = bank if bank is not None else ps()
        pc = slice(d * 256, (d + 1) * 256) if bank is not None else slice(0, 256)
        mm(p[:, pc], lrT[0:16, lrrow, t * 128:(t + 1) * 128], wup[0:16, d, :], True, True, [lrT, wup], [p])
        yield
        K.op(dve, lambda: nc.vector.tensor_tensor(tmp[:, :], p[:, pc], bdec[:, d * 256:(d + 1) * 256], ALU.add), [p, bdec], [tmp])
        yield
        K.op(act, lambda: nc.scalar.activation(tmp[:, :], tmp[:, :], AF.Exp, scale=-1.0), [tmp], [tmp])
        yield
        K.op(act, lambda: nc.scalar.activation(nla[:, d * 256:(d + 1) * 256], tmp[:, :], AF.Ln, bias=1.0), [tmp], [nla])
        yield

    def gla_tm_kd(t, d, tri_t, nla, kdtm, bank=None):
        p = bank if bank is not None else ps()
        mm(p[:, 0:256], tri_t[:, :], nla[:, d * 256:(d + 1) * 256], True, True, [tri_t, nla], [p])
        yield
        K.op(act, lambda: nc.scalar.activation(Etm[:], p[:, 0:256], AF.Exp, scale=1.0 / 16.0), [p], [Etm])
        yield
        K.op(dve, lambda: nc.vector.tensor_tensor(kdtm[:], kbtm[:, t, :], Etm[:], ALU.mult), [kbtm, Etm], [kdtm])
        yield

    def gla_u0(t, kdtm, bank=None):
        pu = bank if bank is not None else ps()
        for h in range(4):
            mm(pu[0:64, h * 128:(h + 1) * 128], kdtm[:, h * 64:(h + 1) * 64], vbtm[:, t, h * 128:(h + 1) * 128], True, True, [kdtm, vbtm], [pu])
        return pu

    def bc4(ap):
        return ap.unsqueeze(2).broadcast_to([64, 4, 128])

    def v3(ap):
        return ap.rearrange("p (h c) -> p h c", h=4)

    def state_enter(Sfb):
        K.op(dve, lambda: nc.vector.tensor_tensor(v3(Sf[:]), v3(Rst[:]), bc4(dprev[:]), ALU.mult), [Rst, dprev], [Sf])
        K.op(act, lambda: nc.scalar.copy(Sfb[:], Sf[:]), [Sf], [Sfb])

    def state_leave(pu, dec):
        K.op(dve, lambda: nc.vector.tensor_tensor(Rst[:], pu[0:64, :], Sf[:], ALU.add), [pu, Sf], [Rst])
        K.op(dve, lambda: nc.vector.tensor_copy(dprev[:], dec[:]), [dec], [dprev])

    def staged_gen(order, stages):
        n, ns = len(order), len(stages)
        for step in range(n + ns - 1):
            gens = []
            for k, st in enumerate(stages):
                i = step - k
                if 0 <= i < n:
                    gens.append(st(order[i]))
            while gens:
                for g_ in list(gens):
                    try:
                        next(g_)
                    except StopIteration:
                        gens.remove(g_)
                yield

    def staged(order, stages):
        for _ in staged_gen(order, stages):
            pass

    def interleave(gens):
        gens = list(gens)
        while gens:
            for g_ in list(gens):
                try:
                    next(g_)
                except StopIteration:
                    gens.remove(g_)

    vaug = [view(hbfs[2 + i].h[:, 0:(NTG + 2) * 128].rearrange("p (t c) -> p t c", c=128), f"vaug{i}") for i in range(2)]
    sinksel = sb("sinksel", [1, 128], BF16)
    K.op(dve, lambda: nc.vector.memset(sinksel[:], 0.0), [], [sinksel])
    K.op(dve, lambda: nc.vector.memset(sinksel[0:1, 64:128], 1.0), [], [sinksel])
    xg2 = view(arena1[:, 0:8192].bitcast(F32).rearrange("p (t c) -> p t c", t=NTG), "xg2")
    hT2 = view(arena2[:, 0:4096].rearrange("p (k c) -> p k c", k=8), "hT2")
    P1V = [xg2, hT2, lrT]
    ARV = A1V + [actT, oaT, omT, mergedT, obT, wfo[0], wfo[1]]

    def gsl(g):
        return slice(g * TG, (g + 1) * TG)

    def mark(label):
        MARKS.append((label, getattr(pe, 'total', 0)))

    for l in range(depth):
        if l == 1:
            issue_casts(jobs1, len(jobs1))
        xsrc = x_in if l == 0 else xbuf
        xdst = xbuf if l < depth - 1 else y_out
        K.dma(sp, gmix[:], p_nmix[l:l + 1, :].partition_broadcast(128), [p_nmix], [gmix])
        K.dma(sp, gmem[:], p_nmem[l:l + 1, :].partition_broadcast(128), [p_nmem], [gmem])
        K.dma(sp, bdec[:], p_bdec[l:l + 1, :, :].rearrange("o d c -> o (d c)").partition_broadcast(128), [p_bdec], [bdec])
        if l > 0:
            K.dma(pool, wup[:], p_wup[l].rearrange("d r c -> r d c"), [p_wup], [wup])
        K.dma(sp, qna[:], p_qna[l:l + 1, :].rearrange("o c -> c o"), [p_qna], [qna])
        K.dma(sp, kna[:], p_kna[l:l + 1, :].rearrange("o c -> c o"), [p_kna], [kna])
        K.dma(sp, qnm[:], p_qnm[l:l + 1, :].rearrange("o c -> c o"), [p_qnm], [qnm])
        K.dma(sp, knm[:], p_knm[l:l + 1, :].rearrange("o c -> c o"), [p_knm], [knm])
        K.dma(sp, gng[:], p_gng[l:l + 1, :].rearrange("o c -> c o"), [p_gng], [gng])
        K.dma(sp, sinkraw[:], p_sink[l:l + 1, :].partition_broadcast(64), [p_sink], [sinkraw])
        K.op(act, lambda: nc.scalar.activation(sinkexp[:], sinkraw[:], AF.Exp, bias=-C_A), [sinkraw], [sinkexp])
        for kvh_ in range(2):
            K.op(dve, lambda kvh_=kvh_: nc.vector.tensor_copy(
                sinkrow[0:1, kvh_, :].rearrange("p (h c) -> p h c", h=4),
                sinkexp[0:1, kvh_ * 4:(kvh_ + 1) * 4].unsqueeze(2).broadcast_to([1, 4, 128])), [sinkexp], [sinkrow])

        mark(f'L{l} mem')
        K.dma(sp, xg[:, 0:2, :], mem_in[:, :].rearrange("(t p) c -> p t c", p=128), [mem_in], [xg])
        for t in range(2):
            norm_transpose(xg, xg[:, t, :], gmem, mnT, t * 128, i=t)
        for half in range(2):
            w = wnext()
            K.dma(sp, w[:], wt[l][T_KV0 + half][:, :, :], [wt[l][T_KV0 + half]], [w])
            if half == 0:
                for h in range(4):
                    p = proj_fm(w, h * 128, 128, mnT, 256)
                    qknorm_fm(p, 128, 256, knm, kmT[:, h, :], kmT, 128)
            else:
                for t in range(2):
                    p = proj_tm(w, 0, 512, mnT, t * 128)
                    K.op(act, lambda p=p, t=t: nc.scalar.copy(vm[:, t, :], p[:, :]), [p], [vm])
        K.dma(sp, gffn[:], p_nffn[l:l + 1, :].partition_broadcast(128), [p_nffn], [gffn])
        K.dma(sp, wl[:], wld[l][:, :, :], [wld[l]], [wl])

        mark(f'L{l} P1')
        handoff(ARV, P1V); handoff([f32d], f32d_h)
        wp1 = wnext()
        K.dma(sp, wp1[:], wt[l][T_P1][:, :, :], [wt[l][T_P1]], [wp1])
        wp1v = wnext()
        K.dma(sp, wp1v[:], wt[l][T_VB][:, :, :], [wt[l][T_VB]], [wp1v])
        K.op(dve, lambda: nc.vector.memset(Rst[:], 0.0), [], [Rst])
        K.op(dve, lambda: nc.vector.memset(dprev[:], 1.0), [], [dprev])
        XGs = [xg, xg2]
        HTs = [hT, hT2]

        def p1_xload(g):
            xt = XGs[g % 2]
            K.dma(sp, xt[:], xsrc[gsl(g), :].rearrange("(t p) c -> p t c", p=128), [xsrc], [xt])

        def p1_A(g):
            xt, ht = XGs[g % 2], HTs[g % 2]
            for t in range(NTG):
                norm_chain(xt, xt[:, t, :], gmix, t)
                yield
            for t in range(NTG):
                norm_tr(t, ht, t * 128)
                yield
            K.dma(sp, hTd[:, :, gsl(g)], ht[:], [ht], [hTd])
            yield

        def p1_B(g):
            ht = HTs[g % 2]
            for kvh in range(2):
                p = proj_fm(wp1, kvh * 64, 64, ht, TG)
                yield
                qknorm_fm(p, 64, TG, kna, kaT[0:64, kvh, 0:TG], kaT, 64)
                yield
            K.dma(sp, kaTd[:, :, gsl(g)], kaT[:, :, 0:TG], [kaT], [kaTd])
            for t in range(NTG):
                p = proj_tm(wp1, 128, 128, ht, t * 128)
                K.op(act, lambda p=p, t=t: nc.scalar.copy(va[:, t, :], p[:, 0:128]), [p], [va])
                yield
            K.dma(sp, vad[:, g * NTG:(g + 1) * NTG, :], va[:, 0:NTG, :], [va], [vad])
            p = proj_fm(wl, 16, 16, ht, TG)
            K.op(act, lambda p=p: nc.scalar.copy(lrT[0:16, 1, :], p[0:16, 0:TG]), [p], [lrT])
            yield

            def p1_s1(t):
                p = proj_tm(wp1, 256, 256, ht, t * 128)
                K.op(act, lambda: nc.scalar.copy(kbtm[:, t, :], p[:, 0:256]), [p], [kbtm])
                yield
                p2_ = proj_tm(wp1v, 0, 512, ht, t * 128)
                K.op(act, lambda: nc.scalar.copy(vbtm[:, t, :], p2_[:, :]), [p2_], [vbtm])
                yield
                yield from gla_logits(t, 1, 1, nlas[t % 2])

            def p1_s2(t):
                nla = nlas[t % 2]
                yield from gla_tm_kd(t, 1, triu, nla, kdtms[t % 2])
                pd = ps()
                for h in range(4):
                    mm(pd[0:64, h:h + 1], nla[:, 256 + h * 64:256 + (h + 1) * 64], ones32[:, 0:1], True, True, [nla, ones32], [pd])
                K.op(act, lambda: nc.scalar.activation(decs[t % 2][:], pd[0:64, 0:4], AF.Exp, scale=-1.0 / 16.0), [pd], [decs[t % 2]])
                yield

            def p1_s3(t):
                n = g * NTG + t
                Sfb = Sfbs[t % 2]
                state_enter(Sfb)
                yield
                K.dma(sp, Sbd[:, n, :], Sfb[:], [Sfb], [Sbd])
                pu = gla_u0(t, kdtms[t % 2])
                yield
                state_leave(pu, decs[t % 2])
                yield
            yield from staged_gen(list(range(NTG - 1, -1, -1)), [p1_s1, p1_s2, p1_s3])

        gs_ = list(range(G - 1, -1, -1))
        p1_xload(gs_[0])
        if G > 1:
            p1_xload(gs_[1])
        interleave([p1_A(gs_[0])])
        for i_ in range(1, G):
            if i_ + 1 < G:
                p1_xload(gs_[i_ + 1])
            interleave([p1_B(gs_[i_ - 1]), p1_A(gs_[i_])])
        interleave([p1_B(gs_[-1])])
        handoff(P1V, ARV)

        mark(f'L{l} P2')
        K.op(dve, lambda: nc.vector.memset(Rst[:], 0.0), [], [Rst])
        K.op(dve, lambda: nc.vector.memset(dprev[:], 1.0), [], [dprev])
        for g in range(G):
            K.dma(sp, hT[:], hTd[:, :, gsl(g)], [hTd], [hT])
            lo = max(0, g * TG - 128); hi = min(S, g * TG + TG + 128)
            off = lo - (g * TG - 128)
            K.dma(sp, kaT[:, :, off:off + (hi - lo)], kaTd[:, :, lo:hi], [kaTd], [kaT])
            handoff([actT], A1V); handoff([wfo[0]], [oaT, omT]); handoff([wfo[1]], [mergedT, obT])

            if l == 0:
                issue_casts(jobs1, per_group)
            mark(f'L{l} g{g} inproj')
            w = wnext(); K.dma(sp, w[:], wt[l][T_QA][:, :, :], [wt[l][T_QA]], [w])
            proj_norm_pipelined([((lambda h=h, w=w: proj_fm(w, h * 64, 64, hT, TG)), 64, TG, qna, qaT[0:64, h, :], qaT, 64) for h in range(8)])
            w = wnext(); K.dma(sp, w[:], wt[l][T_QBKB][:, :, :], [wt[l][T_QBKB]], [w])
            for h in range(4):
                p = proj_fm(w, h * 64, 64, hT, TG)
                K.op(act, lambda p=p, h=h: nc.scalar.copy(qbT32[:, h, :], p[0:64, 0:TG]), [p], [qbT32])
            for h in range(4):
                p = proj_fm(w, 256 + h * 64, 64, hT, TG)
                K.op(act, lambda p=p, h=h: nc.scalar.copy(kbT32[:, h, :], p[0:64, 0:TG]), [p], [kbT32])
            for t in range(NTG):
                p = proj_tm(w, 256, 256, hT, t * 128)
                K.op(act, lambda p=p, t=t: nc.scalar.copy(kbtm[:, t, :], p[:, 0:256]), [p], [kbtm])
            w = wnext(); K.dma(sp, w[:], wt[l][T_VB][:, :, :], [wt[l][T_VB]], [w])
            for t in range(NTG):
                p = proj_tm(w, 0, 512, hT, t * 128)
                K.op(act, lambda p=p, t=t: nc.scalar.copy(vbtm[:, t, :], p[:, :]), [p], [vbtm])
            w = wnext(); K.dma(sp, w[:], wt[l][T_GB][:, :, :], [wt[l][T_GB]], [w])
            for c in range(4):
                p = proj_fm(w, c * 128, 128, hT, TG)
                K.op(act, lambda p=p, c=c: nc.scalar.activation(sog[:, c, :], p[:, 0:TG], AF.Silu), [p], [sog])
            for d in range(2):
                p = proj_fm(wl, d * 16, 16, hT, TG)
                K.op(act, lambda p=p, d=d: nc.scalar.copy(lrT[0:16, d, :], p[0:16, 0:TG]), [p], [lrT])
            w = wnext(); K.dma(sp, w[:], wt[l][T_QM][:, :, :], [wt[l][T_QM]], [w])
            proj_norm_pipelined([((lambda h=h, w=w: proj_fm(w, h * 128, 128, hT, TG)), 128, TG, qnm, qmT[:, h, :], qmT, 128) for h in range(4)])

            handoff([hbfs[2], hbfs[3]], vaug)
            for kvh_ in range(2):
                K.op(dve, lambda kvh_=kvh_: nc.vector.memset(vaug[kvh_][:, :, 64:128], 1.0), [], [vaug[kvh_]])
                K.dma(sp, vaug[kvh_][:, off // 128:off // 128 + (hi - lo) // 128, 0:64],
                      vad[:, lo // 128:hi // 128, kvh_ * 64:(kvh_ + 1) * 64], [vad], [vaug[kvh_]])
            mark(f'L{l} g{g} attnA')
            SC = [PS[0], PS[1]]
            ACC = [PS[2], PS[3]]
            GB = [PS[4], PS[5], PS[6]]
            fA = [f32a, f32c]
            items = []
            for t in range(NTG):
                n = g * NTG + t
                for kvh in range(2):
                    js = [j for j in range(3) if 0 <= n - 1 + j < NT]
                    for ji, j in enumerate(js):
                        items.append((t, kvh, j, ji, len(js)))

            def a_s1(i):
                t, kvh, j, ji, nj = items[i]
                pscr = SC[i % len(SC)]
                kc = (t + j) * 128
                mm(pscr[:, :], kaT[0:64, kvh, kc:kc + 128], qaT[0:64, kvh * 4:(kvh + 1) * 4, t * 128:(t + 1) * 128],
                   True, True, [kaT, qaT], [pscr])
                fa = fA[i % 2]
                K.op(dve, lambda: nc.vector.scalar_tensor_tensor(
                    fa[:], pscr[:, :], 0.125, biasT[:, kvh * 3 + j, :], ALU.mult, ALU.add), [pscr, biasT], [fa])
                pt = pT[i % 3]
                K.op(act, lambda: nc.scalar.activation(pt[:], fa[:], AF.Exp, bias=-C_A), [fa], [pt])

            def a_s2(i):
                t, kvh, j, ji, nj = items[i]
                po = ACC[(t * 2 + kvh) % 2]
                pt = pT[i % 3]
                mm(po[:, :], vaug[kvh][:, t + j, :], pt[:], ji == 0, False, [vaug[kvh], pt], [po])
                if ji == nj - 1:
                    mm(po[:, :], sinksel[0:1, :], sinkrow[0:1, kvh, :], False, True, [sinksel, sinkrow], [po])
                    K.op(act, lambda: nc.scalar.activation(f32b[0:64, :], po[64:128, :], AF.Ln), [po], [f32b])
                    K.op(act, lambda: nc.scalar.activation(f32b[0:64, :], f32b[0:64, :], AF.Exp, scale=-1.0), [f32b], [f32b])
                    K.op(dve, lambda: nc.vector.tensor_tensor(
                        oaT[kvh * 64:(kvh + 1) * 64, :, t * 128:(t + 1) * 128], v3(po[0:64, :]), v3(f32b[0:64, :]), ALU.mult), [po, f32b], [oaT])
            LAG = 1

            def attn_gen():
                for i in range(len(items) + LAG):
                    if i < len(items):
                        a_s1(i)
                        yield
                    if i >= LAG:
                        a_s2(i - LAG)
                        yield
                for i in range(len(mitems) + LAG):
                    if i < len(mitems):
                        m_s1(i)
                        yield
                    if i >= LAG:
                        m_s2(i - LAG)
                        yield

            mark(f'L{l} g{g} attnM')
            mitems = [(h, mc) for h in range(4) for mc in range(2)]

            def m_s1(i):
                h, mc = mitems[i]
                pscr = SC[i % len(SC)]
                mm(pscr[:, 0:TG], kmT[:, h, mc * 128:(mc + 1) * 128], qmT[:, h, :], True, True, [kmT, qmT], [pscr])
                pt = pT[i % 3]
                K.op(act, lambda: nc.scalar.activation(pt[:, 0:TG], pscr[:, 0:TG], AF.Exp, bias=-C_M, scale=1.0 / C_M), [pscr], [pt])

            def m_s2(i):
                h, mc = mitems[i]
                po, pden = ACC[0], ACC[1]
                pt = pT[i % 3]
                mm(po[:, 0:TG], vm[:, mc, h * 128:(h + 1) * 128], pt[:, 0:TG], mc == 0, mc == 1, [vm, pt], [po])
                mm(pden[:, 0:TG], ones_bf[:, :], pt[:, 0:TG], mc == 0, mc == 1, [ones_bf, pt], [pden])
                if mc == 1:
                    K.op(act, lambda: nc.scalar.activation(f32b[:, 0:TG], pden[:, 0:TG], AF.Ln), [pden], [f32b])
                    K.op(act, lambda: nc.scalar.activation(f32b[:, 0:TG], f32b[:, 0:TG], AF.Exp, scale=-1.0), [f32b], [f32b])
                    K.op(dve, lambda: nc.vector.tensor_tensor(omT[:, h, :], po[:, 0:TG], f32b[:, 0:TG], ALU.mult), [po, f32b], [omT])

            handoff([f32d], f32d_h)
            mark(f'L{l} g{g} gla')
            def g_s1(t):
                ga = gla_logits(t, 0, 0, nlas[t % 2], bank=GB[0])
                gb = gla_logits(t, 1, 1, nlas[t % 2], bank=GB[0])
                for _ in ga:
                    next(gb, None)
                    yield

            def g_s2(t):
                n = g * NTG + t
                K.dma(sp, Sblc[n % 2][:], Sbd[:, n, :], [Sbd], [Sblc[n % 2]])
                nla = nlas[t % 2]
                qd, kd, Am = qds[t % 2], kds[t % 2], Ams[t % 2]
                for d in range(2):
                    tri_t = tril if d == 0 else triu
                    pb = GB[1]
                    for h in range(4):
                        mm(pb[0:64, h * 128:(h + 1) * 128], nla[:, d * 256 + h * 64:d * 256 + (h + 1) * 64], tri_t[:, :], True, True, [nla, tri_t], [pb])
                    yield
                    K.op(act, lambda: nc.scalar.activation(Eq[d][:], pb[0:64, :], AF.Exp, scale=-1.0 / 16.0), [pb], [Eq[d]])
                    yield
                    K.op(act, lambda: nc.scalar.activation(Ek[d][:], pb[0:64, :], AF.Exp, scale=1.0 / 16.0), [pb], [Ek[d]])
                    yield
                    K.op(dve, lambda: nc.vector.scalar_tensor_tensor(
                        v3(qd[d][:]), qbT32[:, :, t * 128:(t + 1) * 128], 0.125, v3(Eq[d][:]), ALU.mult, ALU.mult), [qbT32, Eq[d]], [qd[d]])
                    yield
                    K.op(dve, lambda: nc.vector.tensor_tensor(
                        v3(kd[d][:]), kbT32[:, :, t * 128:(t + 1) * 128], v3(Ek[d][:]), ALU.mult), [kbT32, Ek[d]], [kd[d]])
                    if d == 0:
                        K.op(dve, lambda: nc.vector.tensor_copy(decs[t % 2][:], v3(Eq[0][:])[:, :, 127]), [Eq[0]], [decs[t % 2]])
                    yield
                    pa = GB[1]
                    for h in range(4):
                        mm(pa[:, h * 128:(h + 1) * 128], kd[d][:, h * 128:(h + 1) * 128], qd[d][:, h * 128:(h + 1) * 128], True, True, [kd[d], qd[d]], [pa])
                    yield
                    mk = mask_f if d == 0 else mask_b
                    K.op(dve, lambda: nc.vector.tensor_tensor(
                        v3(Am[d][:]), v3(pa[:, :]), mk[:].unsqueeze(1).broadcast_to([128, 4, 128]), ALU.mult), [pa, mk], [Am[d]])
                    yield
                yield from gla_tm_kd(t, 0, tril, nla, kdtms[t % 2], bank=GB[1])

            def g_s3(t):
                n = g * NTG + t
                qd, Am, Sbl, Sfb = qds[t % 2], Ams[t % 2], Sblc[n % 2], Sfbs[t % 2]
                state_enter(Sfb)
                yield
                po = GB[2]
                for h in range(4):
                    hs = slice(h * 128, (h + 1) * 128)
                    mm(po[:, hs], vbtm[:, t, hs], Am[0][:, hs], True, False, [vbtm, Am[0]], [po])
                    mm(po[:, hs], vbtm[:, t, hs], Am[1][:, hs], False, False, [vbtm, Am[1]], [po])
                    mm(po[:, hs], Sfb[0:64, hs], qd[0][:, hs], False, False, [Sfb, qd[0]], [po])
                    mm(po[:, hs], Sbl[0:64, hs], qd[1][:, hs], False, True, [Sbl, qd[1]], [po])
                K.op(act, lambda: nc.scalar.copy(obT32[:, :, t * 128:(t + 1) * 128], v3(po[:, :])), [po], [obT32])
                yield
                pu = gla_u0(t, kdtms[t % 2], bank=GB[2])
                yield
                state_leave(pu, decs[t % 2])
                yield
            interleave([attn_gen(), staged_gen(list(range(NTG)), [g_s1, g_s2, g_s3])])
            handoff(f32d_h, [f32d])
            sqs = [sqb, sqb2]
            K.op(act, lambda: nc.scalar.activation(sqs[0][:, 0:TG], obT32[:, 0, :], AF.Square), [obT32], [sqs[0]])
            for h in range(4):
                sq = sqs[h % 2]
                if h < 3:
                    sqn = sqs[(h + 1) % 2]
                    K.op(act, lambda h=h, sqn=sqn: nc.scalar.activation(sqn[:, 0:TG], obT32[:, h + 1, :], AF.Square), [obT32], [sqn])
                p2 = ps()
                mm(p2[:, 0:TG], ones_bf[:, :], sq[:, 0:TG], True, True, [ones_bf, sq], [p2])
                K.op(act, lambda p2=p2: nc.scalar.activation(f32c[:, 0:TG], p2[:, 0:TG], AF.Ln, bias=EPS, scale=1.0 / 128.0), [p2], [f32c])
                K.op(act, lambda: nc.scalar.activation(f32c[:, 0:TG], f32c[:, 0:TG], AF.Exp, scale=-0.5), [f32c], [f32c])
                K.op(dve, lambda h=h: nc.vector.scalar_tensor_tensor(f32b[:, 0:TG], obT32[:, h, :], gng[:], f32c[:, 0:TG], ALU.mult, ALU.mult), [obT32, gng, f32c], [f32b])
                K.op(dve, lambda h=h: nc.vector.tensor_tensor(obT[:, h, :], f32b[:, 0:TG], sog[:, h, :], ALU.mult), [f32b, sog], [obT])

            K.dma(sp, xg[:], xsrc[gsl(g), :].rearrange("(t p) c -> p t c", p=128), [xsrc], [xg])
            mark(f'L{l} g{g} merge')
            for c in range(8):
                cs = slice(c * 128, (c + 1) * 128)
                w = wnext()
                K.dma(sp, w[:], wt[l][T_MERGE + c][:, :, :], [wt[l][T_MERGE + c]], [w])
                wa_ = wbA[c % 2]
                K.dma(sp, wa_[:], wmad[l][c][:, :, :], [wmad[l][c]], [wa_])
                for j in range(3):
                    pg = proj_fm(w, j * 128, 128, hT, TG)
                    pp = ps()
                    if j == 0:
                        for h in range(4):
                            mm(pp[:, 0:TG], wa_[:, h, :], oaT[:, h, :], h == 0, h == 3, [wa_, oaT], [pp])
                    else:
                        brT = obT if j == 1 else omT
                        for h in range(4):
                            mm(pp[:, 0:TG], w[:, 4 * (j - 1) + h, 384:512], brT[:, h, :], h == 0, h == 3, [w, brT], [pp])
                    K.op(act, lambda pg=pg: nc.scalar.activation(f32a[:, 0:TG], pg[:, 0:TG], AF.Sigmoid), [pg], [f32a])
                    if j == 0:
                        K.op(dve, lambda pp=pp: nc.vector.tensor_tensor(f32d[:, 0:TG], pp[:, 0:TG], f32a[:, 0:TG], ALU.mult), [pp, f32a], [f32d])
                    else:
                        K.op(dve, lambda pp=pp: nc.vector.tensor_tensor(f32b[:, 0:TG], pp[:, 0:TG], f32a[:, 0:TG], ALU.mult), [pp, f32a], [f32b])
                        if j == 1:
                            K.op(dve, lambda: nc.vector.tensor_tensor(f32d[:, 0:TG], f32d[:, 0:TG], f32b[:, 0:TG], ALU.add), [f32d, f32b], [f32d])
                        else:
                            K.op(dve, lambda c=c: nc.vector.tensor_tensor(mergedT[:, c, :], f32d[:, 0:TG], f32b[:, 0:TG], ALU.add), [f32d, f32b], [mergedT])

            mark(f'L{l} g{g} outproj')
            handoff(vaug, [hbfs[2], hbfs[3]])
            wo = []
            for half in range(2):
                w = wnext()
                K.dma(sp, w[:], wt[l][T_OUT0 + half][:, :, :], [wt[l][T_OUT0 + half]], [w])
                wo.append(w)
            for t in range(NTG):
                for half in range(2):
                    p = ps()
                    for k in range(8):
                        mm(p[:, :], mergedT[:, k, t * 128:(t + 1) * 128], wo[half][:, k, :], k == 0, k == 7, [mergedT, wo[half]], [p])
                    K.op(dve, lambda p=p, t=t, half=half: nc.vector.tensor_tensor(
                        xg[:, t, half * 512:(half + 1) * 512], p[:, :], xg[:, t, half * 512:(half + 1) * 512], ALU.add), [p, xg], [xg])
                norm_chain(xg, xg[:, t, :], gffn, t)
                if t >= 1:
                    norm_tr(t - 1, hT, (t - 1) * 128)
            mark(f'L{l} g{g} ffn')
            norm_tr(NTG - 1, hT, (NTG - 1) * 128)
            handoff(A1V, [actT])
            for fc in range(11):
                w = wnext()
                K.dma(sp, w[:], wt[l][T_FFI + fc][:, :, :], [wt[l][T_FFI + fc]], [w])
                for fi in range(2):
                    f = fc * 2 + fi
                    pg = proj_fm(w, fi * 128, 128, hT, TG)
                    pu = proj_fm(w, 256 + fi * 128, 128, hT, TG)
                    K.op(act, lambda pg=pg: nc.scalar.activation(f32a[:, 0:TG], pg[:, 0:TG], AF.Silu), [pg], [f32a])
                    K.op(dve, lambda pu=pu, f=f: nc.vector.tensor_tensor(actT[:, f, :], pu[:, 0:TG], f32a[:, 0:TG], ALU.mult), [pu, f32a], [actT])
            handoff([oaT, omT], [wfo[0]]); handoff([mergedT, obT], [wfo[1]])
            for half in range(2):
                pacc = [ps() for _ in range(NTG)]
                for kh in range(2):
                    wf = wfo[kh]
                    K.dma(sp, wf[:], wfod[l][half * 2 + kh][:, :, :], [wfod[l][half * 2 + kh]], [wf])
                    for t in range(NTG):
                        for f in range(11):
                            mm(pacc[t][:, :], actT[:, kh * 11 + f, t * 128:(t + 1) * 128], wf[:, f, :], kh == 0 and f == 0, kh == 1 and f == 10, [actT, wf], [pacc[t]])
                for t in range(NTG):
                    pq = pacc[t]
                    K.op(dve, lambda t=t, half=half, pq=pq: nc.vector.tensor_tensor(
                        xg[:, t, half * 512:(half + 1) * 512], pq[:, :], xg[:, t, half * 512:(half + 1) * 512], ALU.add), [pq, xg], [xg])
            K.dma(pool, xdst[gsl(g), :].rearrange("(t p) c -> p t c", p=128), xg[:], [xg], [xdst])

    assert not jobs1 or depth == 1 or True
    mark('end')
    K.finish()
    es.close()
    return nc


def t5_bucket_np(rel):
    nb = 16
    max_exact = 8
    ret = (rel > 0).astype(np.int32) * nb
    n = np.abs(rel)
    nf = np.maximum(n, 1).astype(np.float32)
    large = max_exact + (np.log(nf / max_exact) / math.log(128 / max_exact) * (nb - max_exact)).astype(np.int32)
    large = np.minimum(large, nb - 1)
    return ret + np.where(n < max_exact, n, large)


def bias_layout(rel_bias):
    kk = np.arange(128)[:, None, None]
    j = np.arange(3)[None, :, None]
    q = np.arange(128)[None, None, :]
    rel = (j * 128 + kk) - 128 - q
    idx = t5_bucket_np(rel)
    gathered = np.asarray(rel_bias)[idx]
    gathered = gathered.reshape(128, 3, 128, 2, 4).transpose(0, 3, 1, 4, 2)
    return np.ascontiguousarray(gathered.reshape(128, 6, 512)).astype(np.float32)


_NAMES = ["norm_mix_g", "norm_ffn_g", "norm_mem_g", "w_in", "q_norm_a", "k_norm_a", "sink_a", "w_decay_up", "b_decay",
          "gla_norm_g", "w_mem_kv", "q_norm_m", "k_norm_m", "w_branch", "w_out", "w_ffn_in", "w_ffn_out"]


def kernel(**inputs):
    x = np.asarray(inputs["x"], dtype=np.float32)
    mem = np.asarray(inputs["mem"], dtype=np.float32)
    B, S, _ = x.shape
    nc = build(S)
    shared = {n: np.ascontiguousarray(np.asarray(inputs[n], dtype=np.float32)) for n in _NAMES}
    shared["biasg"] = bias_layout(inputs["rel_bias"])
    in_maps = []
    for b in range(B):
        m = dict(shared)
        m["x"] = np.ascontiguousarray(x[b])
        m["mem"] = np.ascontiguousarray(mem[b])
        in_maps.append(m)
    res = run_bass_kernel_spmd(nc, in_maps, core_ids=list(range(B)))
    out = np.stack([np.asarray(r["y"], dtype=np.float32) for r in res.results], axis=0)
    return out
```

```python
import math
from contextlib import ExitStack
import numpy as np
import concourse.bass as bass
import concourse.mybir as mybir
from concourse.bass_utils import run_bass_kernel_spmd

F32 = mybir.dt.float32
BF16 = mybir.dt.bfloat16
ALU = mybir.AluOpType
AF = mybir.ActivationFunctionType

D = 1024
D_IN = 5920
D_FF = 2816
N_MEM = 256
EPS = 1e-6
SEM_LIMIT = 30000
MARKS = []
NLANES = 12
C_A = 8.0
C_M = math.sqrt(128.0)
O_QA, O_KA, O_VA, O_QB, O_KB, O_VB, O_GB, O_LR, O_QM, O_GL = 0, 512, 640, 768, 1024, 1280, 1792, 2304, 2336, 2848


class Buf:
    def __init__(self, name):
        self.name = name
        self.w = []
        self.r = {}


class T:
    def __init__(self, h, name):
        self.h = h
        self.b = Buf(name)

    def __getitem__(self, k):
        return self.h[k]


class Eng:
    def __init__(self, kern, name, h):
        self.k = kern
        self.name = name
        self.h = h
        self.seen = {}
        self.nsem = 0
        self.newsem()
        self.lanes = None
        self.li = 0

    def newsem(self):
        self.sem = self.k.new_sem(f"e_{self.name}_{self.nsem}")
        self.nsem += 1
        self.cnt = 0

    def wait(self, tok):
        sem, val, _ = tok
        if self.seen.get(sem, 0) >= val:
            return
        self.h.wait_ge(sem, val)
        self.seen[sem] = val


class Kern:
    def __init__(self, nc, es):
        self.nc = nc
        self.es = es
        self.nsems = 0
        self.pe = Eng(self, "pe", nc.tensor)
        self.act = Eng(self, "act", nc.scalar)
        self.dve = Eng(self, "dve", nc.vector)
        self.pool = Eng(self, "pool", nc.gpsimd)
        self.sp = Eng(self, "sp", nc.sync)
        for e in (self.sp, self.pool):
            e.lanes = [[self.new_sem(f"l_{e.name}_{i}"), 0] for i in range(NLANES)]

    def new_sem(self, name):
        self.nsems += 1
        return self.es.enter_context(self.nc.semaphore(f"{name}_{self.nsems}"))

    def _deps(self, eng, reads, writes, join=False):
        for t in reads:
            for tok in t.b.w:
                eng.wait(tok)
        for t in writes:
            b = t.b
            for tok in b.w:
                if eng.name == "pe" and tok[2] == "pe":
                    continue
                if join and tok[2].startswith("dma:"):
                    continue
                eng.wait(tok)
            for tok in b.r.values():
                if not (tok[2] == eng.name and eng.name == "pe"):
                    eng.wait(tok)

    def _update(self, key, tok, reads, writes, join=False):
        for t in writes:
            if join:
                t.b.w = [w_ for w_ in t.b.w if w_[2].startswith("dma:")] + [tok]
            else:
                t.b.w = [tok]
            t.b.r = {}
        for t in reads:
            if tok not in t.b.w:
                t.b.r[key] = tok

    def op(self, eng, fn, reads=(), writes=()):
        self._deps(eng, reads, writes)
        ins = fn()
        eng.cnt += 1
        eng.total = getattr(eng, 'total', 0) + 1
        ins.then_inc(eng.sem, 1)
        tok = (eng.sem, eng.cnt, eng.name)
        self._update(eng.name, tok, reads, writes)
        if eng.cnt >= SEM_LIMIT:
            eng.newsem()
        return tok

    def dma(self, eng, out, in_, reads=(), writes=(), join=False):
        self._deps(eng, reads, writes, join)
        li = eng.li % NLANES
        eng.li += 1
        lane = eng.lanes[li]
        if lane[1] >= SEM_LIMIT:
            eng.wait((lane[0], lane[1], "x"))
            lane[0] = self.new_sem(f"l_{eng.name}_{li}")
            lane[1] = 0
        if lane[1] > 0:
            eng.wait((lane[0], lane[1], "x"))
        ins = eng.h.dma_start(out=out, in_=in_)
        lane[1] += 16
        ins.then_inc(lane[0], 16)
        key = f"dma:{eng.name}:{li}"
        tok = (lane[0], lane[1], key)
        self._update(key, tok, reads, writes, join)
        return tok

    def finish(self):
        for e in (self.sp, self.pool):
            for lane in e.lanes:
                if lane[1] > 0:
                    e.wait((lane[0], lane[1], "x"))


def build(S, depth=2, TG=512):
    NT = S // 128
    G = S // TG
    NTG = TG // 128
    nc = bass.Bass("TRN2", target_bir_lowering=False)
    es = ExitStack()
    es.enter_context(nc.allow_low_precision("bf16 matmul operands, fp32 accumulation"))
    es.enter_context(nc.allow_non_contiguous_dma("small strided param loads"))
    K = Kern(nc, es)
    pe, act, dve, pool, sp = K.pe, K.act, K.dve, K.pool, K.sp

    def din(name, shape, dt=F32):
        return T(nc.dram_tensor(name, list(shape), dt, kind="ExternalInput").ap(), name)

    def dscr(name, shape, dt):
        return T(nc.dram_tensor(name, list(shape), dt).ap(), name)

    x_in = din("x", [S, D])
    mem_in = din("mem", [N_MEM, D])
    biasg = din("biasg", [128, 6, 512])
    p_nmix = din("norm_mix_g", [2, D]); p_nffn = din("norm_ffn_g", [2, D]); p_nmem = din("norm_mem_g", [2, D])
    w_in = din("w_in", [2, D, D_IN])
    p_qna = din("q_norm_a", [2, 64]); p_kna = din("k_norm_a", [2, 64]); p_sink = din("sink_a", [2, 8])
    p_wup = din("w_decay_up", [2, 2, 16, 256]); p_bdec = din("b_decay", [2, 2, 256]); p_gng = din("gla_norm_g", [2, 128])
    w_kv = din("w_mem_kv", [2, D, D])
    p_qnm = din("q_norm_m", [2, 128]); p_knm = din("k_norm_m", [2, 128])
    w_br = din("w_branch", [2, 3, 512, D]); w_out = din("w_out", [2, D, D])
    w_fi = din("w_ffn_in", [2, D, 2 * D_FF]); w_fo = din("w_ffn_out", [2, D_FF, D])
    y_out = T(nc.dram_tensor("y", [S, D], F32, kind="ExternalOutput").ap(), "y")

    T_QA, T_QBKB, T_VB, T_GB, T_QM, T_P1, T_KV0, T_KV1, T_OUT0, T_OUT1, T_MERGE, T_FFI, NT_W = 0, 1, 2, 3, 4, 5, 6, 7, 8, 9, 10, 18, 29
    WT_h = nc.dram_tensor("WT", [2, NT_W, 128, 8, 512], BF16).ap()
    WFO_h = nc.dram_tensor("WFO", [2, 4, 128, 11, 512], BF16).ap()
    WMA_h = nc.dram_tensor("WMA", [2, 8, 128, 4, 128], BF16).ap()
    WL_h = nc.dram_tensor("WL", [2, 128, 8, 32], BF16).ap()
    wt = [[T(WT_h[l, t], f"WT{l}_{t}") for t in range(NT_W)] for l in range(2)]
    wfod = [[T(WFO_h[l, i], f"WFO{l}_{i}") for i in range(4)] for l in range(2)]
    wmad = [[T(WMA_h[l, c], f"WMA{l}_{c}") for c in range(8)] for l in range(2)]
    wld = [T(WL_h[l], f"WL{l}") for l in range(2)]
    xbuf = dscr("xbuf", [S, D], F32)
    hTd = dscr("hTd", [128, 8, S], BF16)
    kaTd = dscr("kaTd", [64, 2, S], BF16)
    vad = dscr("vad", [128, NT, 128], BF16)
    Sbd = dscr("Sbd", [64, NT, 512], BF16)

    def sb(name, shape, dt):
        return T(es.enter_context(nc.sbuf_tensor(name, list(shape), dt)), name)

    def psum(name, shape, dt):
        return T(es.enter_context(nc.psum_tensor(name, list(shape), dt)), name)

    PS = [psum(f"ps{i}", [128, 512], F32) for i in range(7)]
    PST = psum("pst", [128, 1024], BF16)
    psi = [0]

    def ps():
        p = PS[psi[0] % len(PS)]
        psi[0] += 1
        return p

    def mm(out, lhsT, rhs, start, stop, reads, writes):
        K.op(pe, lambda: nc.tensor.matmul(out, lhsT, rhs, start=start, stop=stop), reads, writes)

    ident = sb("ident", [128, 128], BF16)
    ones_bf = sb("ones_bf", [128, 128], BF16)
    ones32 = sb("ones32", [128, 128], F32)
    tril = sb("tril", [128, 128], F32)
    triu = sb("triu", [128, 128], F32)
    mask_f = sb("mask_f", [128, 128], F32)
    mask_b = sb("mask_b", [128, 128], F32)
    biasT = sb("biasT", [128, 6, 512], F32)
    K.op(pool, lambda: nc.gpsimd.memset(ones32[:], 1.0), [], [ones32])
    K.op(pool, lambda: nc.gpsimd.memset(ones_bf[:], 1.0), [], [ones_bf])
    K.op(pool, lambda: nc.gpsimd.affine_select(ident[:], ones_bf[:], [[-1, 128]], ALU.is_equal, 0.0, base=0, channel_multiplier=1), [ones_bf], [ident])
    K.op(pool, lambda: nc.gpsimd.affine_select(tril[:], ones32[:], [[1, 128]], ALU.is_ge, 0.0, base=0, channel_multiplier=-1), [ones32], [tril])
    K.op(pool, lambda: nc.gpsimd.affine_select(triu[:], ones32[:], [[-1, 128]], ALU.is_ge, 0.0, base=0, channel_multiplier=1), [ones32], [triu])
    K.op(pool, lambda: nc.gpsimd.memset(mask_f[:], 1.0), [], [mask_f])
    K.op(pool, lambda: nc.gpsimd.memset(mask_b[:], 1.0), [], [mask_b])
    K.op(pool, lambda: nc.gpsimd.affine_select(mask_f[:], mask_f[:], [[1, 128]], ALU.is_ge, 0.0, base=0, channel_multiplier=-1), [mask_f], [mask_f])
    K.op(pool, lambda: nc.gpsimd.affine_select(mask_b[:], mask_b[:], [[-1, 128]], ALU.is_ge, 0.0, base=-1, channel_multiplier=1), [mask_b], [mask_b])
    K.dma(sp, biasT[:], biasg[:], [biasg], [biasT])
    for kvh in range(2):
        v0 = biasT[:, kvh * 3 + 0, :].rearrange("p (h c) -> p h c", h=4)
        K.op(pool, lambda v0=v0: nc.gpsimd.affine_select(v0, v0, [[0, 4], [-1, 128]], ALU.is_ge, -1e30, base=0, channel_multiplier=1), [biasT], [biasT])
        v2 = biasT[:, kvh * 3 + 2, :].rearrange("p (h c) -> p h c", h=4)
        K.op(pool, lambda v2=v2: nc.gpsimd.affine_select(v2, v2, [[0, 4], [1, 128]], ALU.is_ge, -1e30, base=0, channel_multiplier=-1), [biasT], [biasT])

    def cast_jobs(l):
        J = []

        def rows(src, c0, n):
            return src[l, :, c0:c0 + n].rearrange("(k p) c -> p k c", p=128)

        def add(t, d0, src, c0, n):
            J.append((wt[l][t], wt[l][t][:, :, d0:d0 + n], src, rows(src, c0, n)))
        add(T_KV0, 0, w_kv, 0, 512); add(T_KV1, 0, w_kv, 512, 512)
        add(T_P1, 0, w_in, O_KA, 256); add(T_P1, 256, w_in, O_KB, 256)
        add(T_VB, 0, w_in, O_VB, 512)
        J.append((wld[l], wld[l][:, :, :], w_in, rows(w_in, O_LR, 32)))
        add(T_QA, 0, w_in, O_QA, 512); add(T_QBKB, 0, w_in, O_QB, 512); add(T_GB, 0, w_in, O_GB, 512); add(T_QM, 0, w_in, O_QM, 512)
        for c in range(8):
            cs = slice(c * 128, (c + 1) * 128)
            for kvh in range(2):
                J.append((wmad[l][c], wmad[l][c][kvh * 64:(kvh + 1) * 64, :, :], w_br,
                          w_br[l, 0, kvh * 256:(kvh + 1) * 256, cs].rearrange("(h p) c -> p h c", p=64)))
            for j in range(3):
                add(T_MERGE + c, j * 128, w_in, O_GL + j * D + c * 128, 128)
            for j in (1, 2):
                J.append((wt[l][T_MERGE + c], wt[l][T_MERGE + c][:, 4 * (j - 1):4 * j, 384:512], w_br,
                          w_br[l, j, :, cs].rearrange("(h p) c -> p h c", p=128)))
        add(T_OUT0, 0, w_out, 0, 512); add(T_OUT1, 0, w_out, 512, 512)
        for fc in range(11):
            for u in range(2):
                add(T_FFI + fc, u * 256, w_fi, u * D_FF + fc * 256, 256)
        for half in range(2):
            for kh in range(2):
                J.append((wfod[l][half * 2 + kh], wfod[l][half * 2 + kh][:, :, :], w_fo,
                          w_fo[l, kh * 1408:(kh + 1) * 1408, half * 512:(half + 1) * 512].rearrange("(k p) c -> p k c", p=128)))
        return J

    def issue_casts(jobs, n):
        for _ in range(min(n, len(jobs))):
            dst_t, dst_ap, src_t, src_ap = jobs.pop(0)
            K.dma(pool, dst_ap, src_ap, [src_t], [dst_t], join=True)

    wup = sb("wup", [16, 2, 256], BF16)
    jobs0 = cast_jobs(0)
    K.dma(pool, wup[:], p_wup[0].rearrange("d r c -> r d c"), [p_wup], [wup])
    issue_casts(jobs0, len(jobs0))
    jobs1 = cast_jobs(1) if depth > 1 else []
    per_group = -(-len(jobs1) // max(1, G - 1)) if G > 1 else len(jobs1)

    gmix = sb("gmix", [128, D], F32); gffn = sb("gffn", [128, D], F32); gmem = gffn
    bdec = sb("bdec", [128, 512], F32)
    qna = sb("qna", [64, 1], F32); kna = sb("kna", [64, 1], F32)
    qnm = sb("qnm", [128, 1], F32); knm = sb("knm", [128, 1], F32); gng = sb("gng", [128, 1], F32)
    sinkraw = sb("sinkraw", [64, 8], F32); sinkexp = sb("sinkexp", [64, 8], F32); sinkrow = sb("sinkrow", [1, 2, 512], BF16)
    kmT = sb("kmT", [128, 4, 256], BF16); vm = sb("vm", [128, 2, 512], BF16)
    xg = sb("xg", [128, NTG, D], F32)
    hT = sb("hT", [128, 8, TG], BF16)
    hbfs = [sb(f"hbf{i}", [128, D], BF16) for i in range(4)]
    st1s = [sb(f"st1_{i}", [128, 1], F32) for i in range(4)]
    st2s = [sb(f"st2_{i}", [128, 1], F32) for i in range(4)]
    wA = [sb(f"wA{i}", [128, 8, 512], BF16) for i in range(3)]
    wai = [0]
    wl = sb("wl", [128, 8, 32], BF16)
    def view(ap, name):
        return T(ap, name)

    assert TG == 512
    arena1 = es.enter_context(nc.sbuf_tensor("arena1", [128, 11264], BF16))
    actT = view(arena1[:, :].rearrange("p (f c) -> p f c", f=22), "actT")
    qaT = view(arena1[0:64, 0:4096].rearrange("p (h c) -> p h c", h=8), "qaT")
    qbT32 = view(arena1[0:64, 4096:6144].rearrange("p (h c) -> p h c", h=4), "qbT")
    kbT32 = view(arena1[0:64, 6144:8192].rearrange("p (h c) -> p h c", h=4), "kbT")
    sog = view(arena1[:, 8192:10240].rearrange("p (h c) -> p h c", h=4), "sog")
    lrT = view(arena1[0:16, 10240:11264].rearrange("p (d c) -> p d c", d=2), "lrT")
    A1V = [qaT, qbT32, kbT32, sog, lrT]
    arena2 = es.enter_context(nc.sbuf_tensor("arena2", [128, 6144], BF16))
    arena3 = es.enter_context(nc.sbuf_tensor("arena3", [128, 6144], BF16))
    wfo = [view(arena2[:, 0:5632].rearrange("p (f c) -> p f c", f=11), "wfo0"),
           view(arena3[:, 0:5632].rearrange("p (f c) -> p f c", f=11), "wfo1")]
    oaT = view(arena2[:, 0:2048].rearrange("p (h c) -> p h c", h=4), "oaT")
    omT = view(arena2[:, 4096:6144].rearrange("p (h c) -> p h c", h=4), "omT")
    mergedT = view(arena3[:, 0:4096].rearrange("p (k c) -> p k c", k=8), "mergedT")
    obT = view(arena3[:, 4096:6144].rearrange("p (h c) -> p h c", h=4), "obT")

    def handoff(srcs, dsts):
        for d_ in dsts:
            for s_ in srcs:
                if s_ is d_:
                    continue
                for i_, w_ in enumerate(s_.b.w):
                    d_.b.r["h:" + s_.b.name + ":w%d" % i_] = w_
                for k_, tok_ in list(s_.b.r.items()):
                    d_.b.r["h:" + s_.b.name + ":" + k_] = tok_

    kbtm = sb("kbtm", [128, NTG, 256], F32)
    vbtm = sb("vbtm", [128, NTG, 512], BF16)
    qmT = sb("qmT", [128, 4, TG], BF16)
    kaT = sb("kaT", [64, 2, TG + 256], BF16); va = sb("va", [128, NTG + 2, 128], BF16)
    obT32 = sb("obT32", [128, 4, TG], F32)
    f32a = sb("f32a", [128, 512], F32); f32b = sb("f32b", [128, 512], F32); f32c = sb("f32c", [128, 512], F32)
    f32d = sb("f32d", [128, 512], F32)
    sqb = sb("sqb", [128, 512], BF16)
    sqb2 = sb("sqb2", [128, 512], BF16)
    pT = [sb(f"pT{i}", [128, 512], BF16) for i in range(3)]
    nlas = [sb(f"nla{i}", [128, 512], F32) for i in range(2)]
    Eq = [sb(f"Eq{d}", [64, 512], F32) for d in range(2)]
    Ek0 = sb("Ek", [64, 512], F32)
    Ek = [Ek0, Ek0]
    qds = [[sb(f"qd{i}_{d}", [64, 512], BF16) for d in range(2)] for i in range(2)]
    kds = [[sb(f"kd{i}_{d}", [64, 512], BF16) for d in range(2)] for i in range(2)]
    Etm = sb("Etm", [128, 256], F32)
    kdtms = [sb(f"kdtm{i}", [128, 256], BF16) for i in range(2)]
    Ams = [[sb(f"Am{i}_{d}", [128, 512], BF16) for d in range(2)] for i in range(2)]
    decs = [sb(f"dec{i}", [64, 4], F32) for i in range(2)]
    Rst = sb("Rst", [64, 512], F32); Sf = sb("Sf", [64, 512], F32)
    Sfbs = [sb(f"Sfb{i}", [64, 512], BF16) for i in range(2)]
    dprev = sb("dprev", [64, 4], F32)
    Sblc = [sb(f"Sblc{i}", [64, 512], BF16) for i in range(2)]
    wbA = [sb(f"wbA{i}", [128, 4, 128], BF16) for i in range(2)]
    mnT = hT
    print("SBUF bytes remaining per partition:", nc.sbuf_bytes_remaining)

    def wnext():
        w = wA[wai[0] % 3]
        wai[0] += 1
        return w

    def wview(wsc, l, c0, ncols, rows=D):
        return wsc[l, 0:rows, c0:c0 + ncols].rearrange("(k p) c -> p k c", p=128)

    def rsqrt_col(dst, ss, n, parts=128):
        K.op(act, lambda: nc.scalar.activation(dst[0:parts, :], ss[0:parts, :], AF.Sqrt, bias=EPS, scale=1.0 / n), [ss], [dst])
        K.op(dve, lambda: nc.vector.reciprocal(dst[0:parts, :], dst[0:parts, :]), [dst], [dst])

    def norm_chain(src_t, src_ap, gt, i):
        hb, s1_, s2_ = hbfs[i], st1s[i], st2s[i]
        K.op(act, lambda: nc.scalar.activation(hb[:], src_ap, AF.Square, accum_out=s1_[:]), [src_t], [hb, s1_])
        rsqrt_col(s2_, s1_, D)
        K.op(dve, lambda: nc.vector.scalar_tensor_tensor(hb[:], src_ap, s2_[:], gt[:], ALU.mult, ALU.mult), [src_t, s2_, gt], [hb])

    def norm_tr(i, dstT, col0):
        hb = hbfs[i]
        for k in range(8):
            K.op(pe, lambda k=k: nc.tensor.transpose(PST[:, k * 128:(k + 1) * 128], hb[:, k * 128:(k + 1) * 128], ident[:]), [hb, ident], [PST])
        K.op(act, lambda: nc.scalar.copy(dstT[:, :, col0:col0 + 128], PST[:].rearrange("p (k c) -> p k c", k=8)), [PST], [dstT])

    def norm_transpose(src_t, src_ap, gt, dstT, col0, i=0):
        norm_chain(src_t, src_ap, gt, i)
        norm_tr(i, dstT, col0)

    def qknorm_p1(p, parts, ntok):
        K.op(act, lambda: nc.scalar.activation(sqb[0:parts, 0:ntok], p[0:parts, 0:ntok], AF.Square), [p], [sqb])

    def qknorm_p2(p, parts, ntok, gcol, dst_ap, dst_t, n):
        p2 = ps()
        mm(p2[0:parts, 0:ntok], ones_bf[0:parts, 0:parts], sqb[0:parts, 0:ntok], True, True, [ones_bf, sqb], [p2])
        K.op(act, lambda: nc.scalar.activation(f32c[0:parts, 0:ntok], p2[0:parts, 0:ntok], AF.Ln, bias=EPS, scale=1.0 / n), [p2], [f32c])
        K.op(act, lambda: nc.scalar.activation(f32c[0:parts, 0:ntok], f32c[0:parts, 0:ntok], AF.Exp, scale=-0.5), [f32c], [f32c])
        K.op(dve, lambda: nc.vector.scalar_tensor_tensor(dst_ap, p[0:parts, 0:ntok], gcol[0:parts, :], f32c[0:parts, 0:ntok], ALU.mult, ALU.mult), [p, gcol, f32c], [dst_t])

    def qknorm_fm(p, parts, ntok, gcol, dst_ap, dst_t, n):
        qknorm_p1(p, parts, ntok)
        qknorm_p2(p, parts, ntok, gcol, dst_ap, dst_t, n)

    def proj_norm_pipelined(specs):
        pend = None
        for (pf, parts, ntok, gcol, dst_ap, dst_t, n) in specs:
            p = pf()
            if pend is not None:
                qknorm_p2(*pend)
            qknorm_p1(p, parts, ntok)
            pend = (p, parts, ntok, gcol, dst_ap, dst_t, n)
        if pend is not None:
            qknorm_p2(*pend)

    def proj_fm(w, c0, M, rhsT, ntok, nk=8):
        p = ps()
        for k in range(nk):
            mm(p[0:M, 0:ntok], w[:, k, c0:c0 + M], rhsT[:, k, 0:ntok], k == 0, k == nk - 1, [w, rhsT], [p])
        return p

    def proj_tm(w, c0, N, lhsT, t0, nk=8):
        p = ps()
        for k in range(nk):
            mm(p[:, 0:N], lhsT[:, k, t0:t0 + 128], w[:, k, c0:c0 + N], k == 0, k == nk - 1, [w, lhsT], [p])
        return p

    f32d_h = [T(f32d.h[:, 0:256], "f32d_lo"), T(f32d.h[:, 256:512], "f32d_hi")]

    def gla_logits(t, d, lrrow, nla, bank=None):
        tmp = f32d_h[d]
        p = bank if bank is not None else ps()
        pc = slice(d * 256, (d + 1) * 256) if bank is not None else slice(0, 256)
        mm(p[:, pc], lrT[0:16, lrrow, t * 128:(t + 1) * 128], wup[0:16, d, :], True, True, [lrT, wup], [p])
        yield
        K.op(dve, lambda: nc.vector.tensor_tensor(tmp[:, :], p[:, pc], bdec[:, d * 256:(d + 1) * 256], ALU.add), [p, bdec], [tmp])
        yield
        K.op(act, lambda: nc.scalar.activation(tmp[:, :], tmp[:, :], AF.Exp, scale=-1.0), [tmp], [tmp])
        yield
        K.op(act, lambda: nc.scalar.activation(nla[:, d * 256:(d + 1) * 256], tmp[:, :], AF.Ln, bias=1.0), [tmp], [nla])
        yield

    def gla_tm_kd(t, d, tri_t, nla, kdtm, bank=None):
        p = bank if bank is not None else ps()
        mm(p[:, 0:256], tri_t[:, :], nla[:, d * 256:(d + 1) * 256], True, True, [tri_t, nla], [p])
        yield
        K.op(act, lambda: nc.scalar.activation(Etm[:], p[:, 0:256], AF.Exp, scale=1.0 / 16.0), [p], [Etm])
        yield
        K.op(dve, lambda: nc.vector.tensor_tensor(kdtm[:], kbtm[:, t, :], Etm[:], ALU.mult), [kbtm, Etm], [kdtm])
        yield

    def gla_u0(t, kdtm, bank=None):
        pu = bank if bank is not None else ps()
        for h in range(4):
            mm(pu[0:64, h * 128:(h + 1) * 128], kdtm[:, h * 64:(h + 1) * 64], vbtm[:, t, h * 128:(h + 1) * 128], True, True, [kdtm, vbtm], [pu])
        return pu

    def bc4(ap):
        return ap.unsqueeze(2).broadcast_to([64, 4, 128])

    def v3(ap):
        return ap.rearrange("p (h c) -> p h c", h=4)

    def state_enter(Sfb):
        K.op(dve, lambda: nc.vector.tensor_tensor(v3(Sf[:]), v3(Rst[:]), bc4(dprev[:]), ALU.mult), [Rst, dprev], [Sf])
        K.op(act, lambda: nc.scalar.copy(Sfb[:], Sf[:]), [Sf], [Sfb])

    def state_leave(pu, dec):
        K.op(dve, lambda: nc.vector.tensor_tensor(Rst[:], pu[0:64, :], Sf[:], ALU.add), [pu, Sf], [Rst])
        K.op(dve, lambda: nc.vector.tensor_copy(dprev[:], dec[:]), [dec], [dprev])

    def staged_gen(order, stages):
        n, ns = len(order), len(stages)
        for step in range(n + ns - 1):
            gens = []
            for k, st in enumerate(stages):
                i = step - k
                if 0 <= i < n:
                    gens.append(st(order[i]))
            while gens:
                for g_ in list(gens):
                    try:
                        next(g_)
                    except StopIteration:
                        gens.remove(g_)
                yield

    def staged(order, stages):
        for _ in staged_gen(order, stages):
            pass

    def interleave(gens):
        gens = list(gens)
        while gens:
            for g_ in list(gens):
                try:
                    next(g_)
                except StopIteration:
                    gens.remove(g_)

    vaug = [view(hbfs[2 + i].h[:, 0:(NTG + 2) * 128].rearrange("p (t c) -> p t c", c=128), f"vaug{i}") for i in range(2)]
    sinksel = sb("sinksel", [1, 128], BF16)
    K.op(dve, lambda: nc.vector.memset(sinksel[:], 0.0), [], [sinksel])
    K.op(dve, lambda: nc.vector.memset(sinksel[0:1, 64:128], 1.0), [], [sinksel])
    xg2 = view(arena1[:, 0:8192].bitcast(F32).rearrange("p (t c) -> p t c", t=NTG), "xg2")
    hT2 = view(arena2[:, 0:4096].rearrange("p (k c) -> p k c", k=8), "hT2")
    P1V = [xg2, hT2, lrT]
    ARV = A1V + [actT, oaT, omT, mergedT, obT, wfo[0], wfo[1]]

    def gsl(g):
        return slice(g * TG, (g + 1) * TG)

    def mark(label):
        MARKS.append((label, getattr(pe, 'total', 0)))

    for l in range(depth):
        if l == 1:
            issue_casts(jobs1, len(jobs1))
        xsrc = x_in if l == 0 else xbuf
        xdst = xbuf if l < depth - 1 else y_out
        K.dma(sp, gmix[:], p_nmix[l:l + 1, :].partition_broadcast(128), [p_nmix], [gmix])
        K.dma(sp, gmem[:], p_nmem[l:l + 1, :].partition_broadcast(128), [p_nmem], [gmem])
        K.dma(sp, bdec[:], p_bdec[l:l + 1, :, :].rearrange("o d c -> o (d c)").partition_broadcast(128), [p_bdec], [bdec])
        if l > 0:
            K.dma(pool, wup[:], p_wup[l].rearrange("d r c -> r d c"), [p_wup], [wup])
        K.dma(sp, qna[:], p_qna[l:l + 1, :].rearrange("o c -> c o"), [p_qna], [qna])
        K.dma(sp, kna[:], p_kna[l:l + 1, :].rearrange("o c -> c o"), [p_kna], [kna])
        K.dma(sp, qnm[:], p_qnm[l:l + 1, :].rearrange("o c -> c o"), [p_qnm], [qnm])
        K.dma(sp, knm[:], p_knm[l:l + 1, :].rearrange("o c -> c o"), [p_knm], [knm])
        K.dma(sp, gng[:], p_gng[l:l + 1, :].rearrange("o c -> c o"), [p_gng], [gng])
        K.dma(sp, sinkraw[:], p_sink[l:l + 1, :].partition_broadcast(64), [p_sink], [sinkraw])
        K.op(act, lambda: nc.scalar.activation(sinkexp[:], sinkraw[:], AF.Exp, bias=-C_A), [sinkraw], [sinkexp])
        for kvh_ in range(2):
            K.op(dve, lambda kvh_=kvh_: nc.vector.tensor_copy(
                sinkrow[0:1, kvh_, :].rearrange("p (h c) -> p h c", h=4),
                sinkexp[0:1, kvh_ * 4:(kvh_ + 1) * 4].unsqueeze(2).broadcast_to([1, 4, 128])), [sinkexp], [sinkrow])

        mark(f'L{l} mem')
        K.dma(sp, xg[:, 0:2, :], mem_in[:, :].rearrange("(t p) c -> p t c", p=128), [mem_in], [xg])
        for t in range(2):
            norm_transpose(xg, xg[:, t, :], gmem, mnT, t * 128, i=t)
        for half in range(2):
            w = wnext()
            K.dma(sp, w[:], wt[l][T_KV0 + half][:, :, :], [wt[l][T_KV0 + half]], [w])
            if half == 0:
                for h in range(4):
                    p = proj_fm(w, h * 128, 128, mnT, 256)
                    qknorm_fm(p, 128, 256, knm, kmT[:, h, :], kmT, 128)
            else:
                for t in range(2):
                    p = proj_tm(w, 0, 512, mnT, t * 128)
                    K.op(act, lambda p=p, t=t: nc.scalar.copy(vm[:, t, :], p[:, :]), [p], [vm])
        K.dma(sp, gffn[:], p_nffn[l:l + 1, :].partition_broadcast(128), [p_nffn], [gffn])
        K.dma(sp, wl[:], wld[l][:, :, :], [wld[l]], [wl])

        mark(f'L{l} P1')
        handoff(ARV, P1V); handoff([f32d], f32d_h)
        wp1 = wnext()
        K.dma(sp, wp1[:], wt[l][T_P1][:, :, :], [wt[l][T_P1]], [wp1])
        wp1v = wnext()
        K.dma(sp, wp1v[:], wt[l][T_VB][:, :, :], [wt[l][T_VB]], [wp1v])
        K.op(dve, lambda: nc.vector.memset(Rst[:], 0.0), [], [Rst])
        K.op(dve, lambda: nc.vector.memset(dprev[:], 1.0), [], [dprev])
        XGs = [xg, xg2]
        HTs = [hT, hT2]

        def p1_xload(g):
            xt = XGs[g % 2]
            K.dma(sp, xt[:], xsrc[gsl(g), :].rearrange("(t p) c -> p t c", p=128), [xsrc], [xt])

        def p1_A(g):
            xt, ht = XGs[g % 2], HTs[g % 2]
            for t in range(NTG):
                norm_chain(xt, xt[:, t, :], gmix, t)
                yield
            for t in range(NTG):
                norm_tr(t, ht, t * 128)
                yield
            K.dma(sp, hTd[:, :, gsl(g)], ht[:], [ht], [hTd])
            yield

        def p1_B(g):
            ht = HTs[g % 2]
            for kvh in range(2):
                p = proj_fm(wp1, kvh * 64, 64, ht, TG)
                yield
                qknorm_fm(p, 64, TG, kna, kaT[0:64, kvh, 0:TG], kaT, 64)
                yield
            K.dma(sp, kaTd[:, :, gsl(g)], kaT[:, :, 0:TG], [kaT], [kaTd])
            for t in range(NTG):
                p = proj_tm(wp1, 128, 128, ht, t * 128)
                K.op(act, lambda p=p, t=t: nc.scalar.copy(va[:, t, :], p[:, 0:128]), [p], [va])
                yield
            K.dma(sp, vad[:, g * NTG:(g + 1) * NTG, :], va[:, 0:NTG, :], [va], [vad])
            p = proj_fm(wl, 16, 16, ht, TG)
            K.op(act, lambda p=p: nc.scalar.copy(lrT[0:16, 1, :], p[0:16, 0:TG]), [p], [lrT])
            yield

            def p1_s1(t):
                p = proj_tm(wp1, 256, 256, ht, t * 128)
                K.op(act, lambda: nc.scalar.copy(kbtm[:, t, :], p[:, 0:256]), [p], [kbtm])
                yield
                p2_ = proj_tm(wp1v, 0, 512, ht, t * 128)
                K.op(act, lambda: nc.scalar.copy(vbtm[:, t, :], p2_[:, :]), [p2_], [vbtm])
                yield
                yield from gla_logits(t, 1, 1, nlas[t % 2])

            def p1_s2(t):
                nla = nlas[t % 2]
                yield from gla_tm_kd(t, 1, triu, nla, kdtms[t % 2])
                pd = ps()
                for h in range(4):
                    mm(pd[0:64, h:h + 1], nla[:, 256 + h * 64:256 + (h + 1) * 64], ones32[:, 0:1], True, True, [nla, ones32], [pd])
                K.op(act, lambda: nc.scalar.activation(decs[t % 2][:], pd[0:64, 0:4], AF.Exp, scale=-1.0 / 16.0), [pd], [decs[t % 2]])
                yield

            def p1_s3(t):
                n = g * NTG + t
                Sfb = Sfbs[t % 2]
                state_enter(Sfb)
                yield
                K.dma(sp, Sbd[:, n, :], Sfb[:], [Sfb], [Sbd])
                pu = gla_u0(t, kdtms[t % 2])
                yield
                state_leave(pu, decs[t % 2])
                yield
            yield from staged_gen(list(range(NTG - 1, -1, -1)), [p1_s1, p1_s2, p1_s3])

        gs_ = list(range(G - 1, -1, -1))
        p1_xload(gs_[0])
        if G > 1:
            p1_xload(gs_[1])
        interleave([p1_A(gs_[0])])
        for i_ in range(1, G):
            if i_ + 1 < G:
                p1_xload(gs_[i_ + 1])
            interleave([p1_B(gs_[i_ - 1]), p1_A(gs_[i_])])
        interleave([p1_B(gs_[-1])])
        handoff(P1V, ARV)

        mark(f'L{l} P2')
        K.op(dve, lambda: nc.vector.memset(Rst[:], 0.0), [], [Rst])
        K.op(dve, lambda: nc.vector.memset(dprev[:], 1.0), [], [dprev])
        for g in range(G):
            K.dma(sp, hT[:], hTd[:, :, gsl(g)], [hTd], [hT])
            lo = max(0, g * TG - 128); hi = min(S, g * TG + TG + 128)
            off = lo - (g * TG - 128)
            K.dma(sp, kaT[:, :, off:off + (hi - lo)], kaTd[:, :, lo:hi], [kaTd], [kaT])
            handoff([actT], A1V); handoff([wfo[0]], [oaT, omT]); handoff([wfo[1]], [mergedT, obT])

            if l == 0:
                issue_casts(jobs1, per_group)
            mark(f'L{l} g{g} inproj')
            w = wnext(); K.dma(sp, w[:], wt[l][T_QA][:, :, :], [wt[l][T_QA]], [w])
            proj_norm_pipelined([((lambda h=h, w=w: proj_fm(w, h * 64, 64, hT, TG)), 64, TG, qna, qaT[0:64, h, :], qaT, 64) for h in range(8)])
            w = wnext(); K.dma(sp, w[:], wt[l][T_QBKB][:, :, :], [wt[l][T_QBKB]], [w])
            for h in range(4):
                p = proj_fm(w, h * 64, 64, hT, TG)
                K.op(act, lambda p=p, h=h: nc.scalar.copy(qbT32[:, h, :], p[0:64, 0:TG]), [p], [qbT32])
            for h in range(4):
                p = proj_fm(w, 256 + h * 64, 64, hT, TG)
                K.op(act, lambda p=p, h=h: nc.scalar.copy(kbT32[:, h, :], p[0:64, 0:TG]), [p], [kbT32])
            for t in range(NTG):
                p = proj_tm(w, 256, 256, hT, t * 128)
                K.op(act, lambda p=p, t=t: nc.scalar.copy(kbtm[:, t, :], p[:, 0:256]), [p], [kbtm])
            w = wnext(); K.dma(sp, w[:], wt[l][T_VB][:, :, :], [wt[l][T_VB]], [w])
            for t in range(NTG):
                p = proj_tm(w, 0, 512, hT, t * 128)
                K.op(act, lambda p=p, t=t: nc.scalar.copy(vbtm[:, t, :], p[:, :]), [p], [vbtm])
            w = wnext(); K.dma(sp, w[:], wt[l][T_GB][:, :, :], [wt[l][T_GB]], [w])
            for c in range(4):
                p = proj_fm(w, c * 128, 128, hT, TG)
                K.op(act, lambda p=p, c=c: nc.scalar.activation(sog[:, c, :], p[:, 0:TG], AF.Silu), [p], [sog])
            for d in range(2):
                p = proj_fm(wl, d * 16, 16, hT, TG)
                K.op(act, lambda p=p, d=d: nc.scalar.copy(lrT[0:16, d, :], p[0:16, 0:TG]), [p], [lrT])
            w = wnext(); K.dma(sp, w[:], wt[l][T_QM][:, :, :], [wt[l][T_QM]], [w])
            proj_norm_pipelined([((lambda h=h, w=w: proj_fm(w, h * 128, 128, hT, TG)), 128, TG, qnm, qmT[:, h, :], qmT, 128) for h in range(4)])

            handoff([hbfs[2], hbfs[3]], vaug)
            for kvh_ in range(2):
                K.op(dve, lambda kvh_=kvh_: nc.vector.memset(vaug[kvh_][:, :, 64:128], 1.0), [], [vaug[kvh_]])
                K.dma(sp, vaug[kvh_][:, off // 128:off // 128 + (hi - lo) // 128, 0:64],
                      vad[:, lo // 128:hi // 128, kvh_ * 64:(kvh_ + 1) * 64], [vad], [vaug[kvh_]])
            mark(f'L{l} g{g} attnA')
            SC = [PS[0], PS[1]]
            ACC = [PS[2], PS[3]]
            GB = [PS[4], PS[5], PS[6]]
            fA = [f32a, f32c]
            items = []
            for t in range(NTG):
                n = g * NTG + t
                for kvh in range(2):
                    js = [j for j in range(3) if 0 <= n - 1 + j < NT]
                    for ji, j in enumerate(js):
                        items.append((t, kvh, j, ji, len(js)))

            def a_s1(i):
                t, kvh, j, ji, nj = items[i]
                pscr = SC[i % len(SC)]
                kc = (t + j) * 128
                mm(pscr[:, :], kaT[0:64, kvh, kc:kc + 128], qaT[0:64, kvh * 4:(kvh + 1) * 4, t * 128:(t + 1) * 128],
                   True, True, [kaT, qaT], [pscr])
                fa = fA[i % 2]
                K.op(dve, lambda: nc.vector.scalar_tensor_tensor(
                    fa[:], pscr[:, :], 0.125, biasT[:, kvh * 3 + j, :], ALU.mult, ALU.add), [pscr, biasT], [fa])
                pt = pT[i % 3]
                K.op(act, lambda: nc.scalar.activation(pt[:], fa[:], AF.Exp, bias=-C_A), [fa], [pt])

            def a_s2(i):
                t, kvh, j, ji, nj = items[i]
                po = ACC[(t * 2 + kvh) % 2]
                pt = pT[i % 3]
                mm(po[:, :], vaug[kvh][:, t + j, :], pt[:], ji == 0, False, [vaug[kvh], pt], [po])
                if ji == nj - 1:
                    mm(po[:, :], sinksel[0:1, :], sinkrow[0:1, kvh, :], False, True, [sinksel, sinkrow], [po])
                    K.op(act, lambda: nc.scalar.activation(f32b[0:64, :], po[64:128, :], AF.Ln), [po], [f32b])
                    K.op(act, lambda: nc.scalar.activation(f32b[0:64, :], f32b[0:64, :], AF.Exp, scale=-1.0), [f32b], [f32b])
                    K.op(dve, lambda: nc.vector.tensor_tensor(
                        oaT[kvh * 64:(kvh + 1) * 64, :, t * 128:(t + 1) * 128], v3(po[0:64, :]), v3(f32b[0:64, :]), ALU.mult), [po, f32b], [oaT])
            LAG = 1

            def attn_gen():
                for i in range(len(items) + LAG):
                    if i < len(items):
                        a_s1(i)
                        yield
                    if i >= LAG:
                        a_s2(i - LAG)
                        yield
                for i in range(len(mitems) + LAG):
                    if i < len(mitems):
                        m_s1(i)
                        yield
                    if i >= LAG:
                        m_s2(i - LAG)
                        yield

            mark(f'L{l} g{g} attnM')
            mitems = [(h, mc) for h in range(4) for mc in range(2)]

            def m_s1(i):
                h, mc = mitems[i]
                pscr = SC[i % len(SC)]
                mm(pscr[:, 0:TG], kmT[:, h, mc * 128:(mc + 1) * 128], qmT[:, h, :], True, True, [kmT, qmT], [pscr])
                pt = pT[i % 3]
                K.op(act, lambda: nc.scalar.activation(pt[:, 0:TG], pscr[:, 0:TG], AF.Exp, bias=-C_M, scale=1.0 / C_M), [pscr], [pt])

            def m_s2(i):
                h, mc = mitems[i]
                po, pden = ACC[0], ACC[1]
                pt = pT[i % 3]
                mm(po[:, 0:TG], vm[:, mc, h * 128:(h + 1) * 128], pt[:, 0:TG], mc == 0, mc == 1, [vm, pt], [po])
                mm(pden[:, 0:TG], ones_bf[:, :], pt[:, 0:TG], mc == 0, mc == 1, [ones_bf, pt], [pden])
                if mc == 1:
                    K.op(act, lambda: nc.scalar.activation(f32b[:, 0:TG], pden[:, 0:TG], AF.Ln), [pden], [f32b])
                    K.op(act, lambda: nc.scalar.activation(f32b[:, 0:TG], f32b[:, 0:TG], AF.Exp, scale=-1.0), [f32b], [f32b])
                    K.op(dve, lambda: nc.vector.tensor_tensor(omT[:, h, :], po[:, 0:TG], f32b[:, 0:TG], ALU.mult), [po, f32b], [omT])

            handoff([f32d], f32d_h)
            mark(f'L{l} g{g} gla')
            def g_s1(t):
                ga = gla_logits(t, 0, 0, nlas[t % 2], bank=GB[0])
                gb = gla_logits(t, 1, 1, nlas[t % 2], bank=GB[0])
                for _ in ga:
                    next(gb, None)
                    yield

            def g_s2(t):
                n = g * NTG + t
                K.dma(sp, Sblc[n % 2][:], Sbd[:, n, :], [Sbd], [Sblc[n % 2]])
                nla = nlas[t % 2]
                qd, kd, Am = qds[t % 2], kds[t % 2], Ams[t % 2]
                for d in range(2):
                    tri_t = tril if d == 0 else triu
                    pb = GB[1]
                    for h in range(4):
                        mm(pb[0:64, h * 128:(h + 1) * 128], nla[:, d * 256 + h * 64:d * 256 + (h + 1) * 64], tri_t[:, :], True, True, [nla, tri_t], [pb])
                    yield
                    K.op(act, lambda: nc.scalar.activation(Eq[d][:], pb[0:64, :], AF.Exp, scale=-1.0 / 16.0), [pb], [Eq[d]])
                    yield
                    K.op(act, lambda: nc.scalar.activation(Ek[d][:], pb[0:64, :], AF.Exp, scale=1.0 / 16.0), [pb], [Ek[d]])
                    yield
                    K.op(dve, lambda: nc.vector.scalar_tensor_tensor(
                        v3(qd[d][:]), qbT32[:, :, t * 128:(t + 1) * 128], 0.125, v3(Eq[d][:]), ALU.mult, ALU.mult), [qbT32, Eq[d]], [qd[d]])
                    yield
                    K.op(dve, lambda: nc.vector.tensor_tensor(
                        v3(kd[d][:]), kbT32[:, :, t * 128:(t + 1) * 128], v3(Ek[d][:]), ALU.mult), [kbT32, Ek[d]], [kd[d]])
                    if d == 0:
                        K.op(dve, lambda: nc.vector.tensor_copy(decs[t % 2][:], v3(Eq[0][:])[:, :, 127]), [Eq[0]], [decs[t % 2]])
                    yield
                    pa = GB[1]
                    for h in range(4):
                        mm(pa[:, h * 128:(h + 1) * 128], kd[d][:, h * 128:(h + 1) * 128], qd[d][:, h * 128:(h + 1) * 128], True, True, [kd[d], qd[d]], [pa])
                    yield
                    mk = mask_f if d == 0 else mask_b
                    K.op(dve, lambda: nc.vector.tensor_tensor(
                        v3(Am[d][:]), v3(pa[:, :]), mk[:].unsqueeze(1).broadcast_to([128, 4, 128]), ALU.mult), [pa, mk], [Am[d]])
                    yield
                yield from gla_tm_kd(t, 0, tril, nla, kdtms[t % 2], bank=GB[1])

            def g_s3(t):
                n = g * NTG + t
                qd, Am, Sbl, Sfb = qds[t % 2], Ams[t % 2], Sblc[n % 2], Sfbs[t % 2]
                state_enter(Sfb)
                yield
                po = GB[2]
                for h in range(4):
                    hs = slice(h * 128, (h + 1) * 128)
                    mm(po[:, hs], vbtm[:, t, hs], Am[0][:, hs], True, False, [vbtm, Am[0]], [po])
                    mm(po[:, hs], vbtm[:, t, hs], Am[1][:, hs], False, False, [vbtm, Am[1]], [po])
                    mm(po[:, hs], Sfb[0:64, hs], qd[0][:, hs], False, False, [Sfb, qd[0]], [po])
                    mm(po[:, hs], Sbl[0:64, hs], qd[1][:, hs], False, True, [Sbl, qd[1]], [po])
                K.op(act, lambda: nc.scalar.copy(obT32[:, :, t * 128:(t + 1) * 128], v3(po[:, :])), [po], [obT32])
                yield
                pu = gla_u0(t, kdtms[t % 2], bank=GB[2])
                yield
                state_leave(pu, decs[t % 2])
                yield
            interleave([attn_gen(), staged_gen(list(range(NTG)), [g_s1, g_s2, g_s3])])
            handoff(f32d_h, [f32d])
            sqs = [sqb, sqb2]
            K.op(act, lambda: nc.scalar.activation(sqs[0][:, 0:TG], obT32[:, 0, :], AF.Square), [obT32], [sqs[0]])
            for h in range(4):
                sq = sqs[h % 2]
                if h < 3:
                    sqn = sqs[(h + 1) % 2]
                    K.op(act, lambda h=h, sqn=sqn: nc.scalar.activation(sqn[:, 0:TG], obT32[:, h + 1, :], AF.Square), [obT32], [sqn])
                p2 = ps()
                mm(p2[:, 0:TG], ones_bf[:, :], sq[:, 0:TG], True, True, [ones_bf, sq], [p2])
                K.op(act, lambda p2=p2: nc.scalar.activation(f32c[:, 0:TG], p2[:, 0:TG], AF.Ln, bias=EPS, scale=1.0 / 128.0), [p2], [f32c])
                K.op(act, lambda: nc.scalar.activation(f32c[:, 0:TG], f32c[:, 0:TG], AF.Exp, scale=-0.5), [f32c], [f32c])
                K.op(dve, lambda h=h: nc.vector.scalar_tensor_tensor(f32b[:, 0:TG], obT32[:, h, :], gng[:], f32c[:, 0:TG], ALU.mult, ALU.mult), [obT32, gng, f32c], [f32b])
                K.op(dve, lambda h=h: nc.vector.tensor_tensor(obT[:, h, :], f32b[:, 0:TG], sog[:, h, :], ALU.mult), [f32b, sog], [obT])

            K.dma(sp, xg[:], xsrc[gsl(g), :].rearrange("(t p) c -> p t c", p=128), [xsrc], [xg])
            mark(f'L{l} g{g} merge')
            for c in range(8):
                cs = slice(c * 128, (c + 1) * 128)
                w = wnext()
                K.dma(sp, w[:], wt[l][T_MERGE + c][:, :, :], [wt[l][T_MERGE + c]], [w])
                wa_ = wbA[c % 2]
                K.dma(sp, wa_[:], wmad[l][c][:, :, :], [wmad[l][c]], [wa_])
                for j in range(3):
                    pg = proj_fm(w, j * 128, 128, hT, TG)
                    pp = ps()
                    if j == 0:
                        for h in range(4):
                            mm(pp[:, 0:TG], wa_[:, h, :], oaT[:, h, :], h == 0, h == 3, [wa_, oaT], [pp])
                    else:
                        brT = obT if j == 1 else omT
                        for h in range(4):
                            mm(pp[:, 0:TG], w[:, 4 * (j - 1) + h, 384:512], brT[:, h, :], h == 0, h == 3, [w, brT], [pp])
                    K.op(act, lambda pg=pg: nc.scalar.activation(f32a[:, 0:TG], pg[:, 0:TG], AF.Sigmoid), [pg], [f32a])
                    if j == 0:
                        K.op(dve, lambda pp=pp: nc.vector.tensor_tensor(f32d[:, 0:TG], pp[:, 0:TG], f32a[:, 0:TG], ALU.mult), [pp, f32a], [f32d])
                    else:
                        K.op(dve, lambda pp=pp: nc.vector.tensor_tensor(f32b[:, 0:TG], pp[:, 0:TG], f32a[:, 0:TG], ALU.mult), [pp, f32a], [f32b])
                        if j == 1:
                            K.op(dve, lambda: nc.vector.tensor_tensor(f32d[:, 0:TG], f32d[:, 0:TG], f32b[:, 0:TG], ALU.add), [f32d, f32b], [f32d])
                        else:
                            K.op(dve, lambda c=c: nc.vector.tensor_tensor(mergedT[:, c, :], f32d[:, 0:TG], f32b[:, 0:TG], ALU.add), [f32d, f32b], [mergedT])

            mark(f'L{l} g{g} outproj')
            handoff(vaug, [hbfs[2], hbfs[3]])
            wo = []
            for half in range(2):
                w = wnext()
                K.dma(sp, w[:], wt[l][T_OUT0 + half][:, :, :], [wt[l][T_OUT0 + half]], [w])
                wo.append(w)
            for t in range(NTG):
                for half in range(2):
                    p = ps()
                    for k in range(8):
                        mm(p[:, :], mergedT[:, k, t * 128:(t + 1) * 128], wo[half][:, k, :], k == 0, k == 7, [mergedT, wo[half]], [p])
                    K.op(dve, lambda p=p, t=t, half=half: nc.vector.tensor_tensor(
                        xg[:, t, half * 512:(half + 1) * 512], p[:, :], xg[:, t, half * 512:(half + 1) * 512], ALU.add), [p, xg], [xg])
                norm_chain(xg, xg[:, t, :], gffn, t)
                if t >= 1:
                    norm_tr(t - 1, hT, (t - 1) * 128)
            mark(f'L{l} g{g} ffn')
            norm_tr(NTG - 1, hT, (NTG - 1) * 128)
            handoff(A1V, [actT])
            for fc in range(11):
                w = wnext()
                K.dma(sp, w[:], wt[l][T_FFI + fc][:, :, :], [wt[l][T_FFI + fc]], [w])
                for fi in range(2):
                    f = fc * 2 + fi
                    pg = proj_fm(w, fi * 128, 128, hT, TG)
                    pu = proj_fm(w, 256 + fi * 128, 128, hT, TG)
                    K.op(act, lambda pg=pg: nc.scalar.activation(f32a[:, 0:TG], pg[:, 0:TG], AF.Silu), [pg], [f32a])
                    K.op(dve, lambda pu=pu, f=f: nc.vector.tensor_tensor(actT[:, f, :], pu[:, 0:TG], f32a[:, 0:TG], ALU.mult), [pu, f32a], [actT])
            handoff([oaT, omT], [wfo[0]]); handoff([mergedT, obT], [wfo[1]])
            for half in range(2):
                pacc = [ps() for _ in range(NTG)]
                for kh in range(2):
                    wf = wfo[kh]
                    K.dma(sp, wf[:], wfod[l][half * 2 + kh][:, :, :], [wfod[l][half * 2 + kh]], [wf])
                    for t in range(NTG):
                        for f in range(11):
                            mm(pacc[t][:, :], actT[:, kh * 11 + f, t * 128:(t + 1) * 128], wf[:, f, :], kh == 0 and f == 0, kh == 1 and f == 10, [actT, wf], [pacc[t]])
                for t in range(NTG):
                    pq = pacc[t]
                    K.op(dve, lambda t=t, half=half, pq=pq: nc.vector.tensor_tensor(
                        xg[:, t, half * 512:(half + 1) * 512], pq[:, :], xg[:, t, half * 512:(half + 1) * 512], ALU.add), [pq, xg], [xg])
            K.dma(pool, xdst[gsl(g), :].rearrange("(t p) c -> p t c", p=128), xg[:], [xg], [xdst])

    assert not jobs1 or depth == 1 or True
    mark('end')
    K.finish()
    es.close()
    return nc


def t5_bucket_np(rel):
    nb = 16
    max_exact = 8
    ret = (rel > 0).astype(np.int32) * nb
    n = np.abs(rel)
    nf = np.maximum(n, 1).astype(np.float32)
    large = max_exact + (np.log(nf / max_exact) / math.log(128 / max_exact) * (nb - max_exact)).astype(np.int32)
    large = np.minimum(large, nb - 1)
    return ret + np.where(n < max_exact, n, large)


def bias_layout(rel_bias):
    kk = np.arange(128)[:, None, None]
    j = np.arange(3)[None, :, None]
    q = np.arange(128)[None, None, :]
    rel = (j * 128 + kk) - 128 - q
    idx = t5_bucket_np(rel)
    gathered = np.asarray(rel_bias)[idx]
    gathered = gathered.reshape(128, 3, 128, 2, 4).transpose(0, 3, 1, 4, 2)
    return np.ascontiguousarray(gathered.reshape(128, 6, 512)).astype(np.float32)


_NAMES = ["norm_mix_g", "norm_ffn_g", "norm_mem_g", "w_in", "q_norm_a", "k_norm_a", "sink_a", "w_decay_up", "b_decay",
          "gla_norm_g", "w_mem_kv", "q_norm_m", "k_norm_m", "w_branch", "w_out", "w_ffn_in", "w_ffn_out"]


def kernel(**inputs):
    x = np.asarray(inputs["x"], dtype=np.float32)
    mem = np.asarray(inputs["mem"], dtype=np.float32)
    B, S, _ = x.shape
    nc = build(S)
    shared = {n: np.ascontiguousarray(np.asarray(inputs[n], dtype=np.float32)) for n in _NAMES}
    shared["biasg"] = bias_layout(inputs["rel_bias"])
    in_maps = []
    for b in range(B):
        m = dict(shared)
        m["x"] = np.ascontiguousarray(x[b])
        m["mem"] = np.ascontiguousarray(mem[b])
        in_maps.append(m)
    res = run_bass_kernel_spmd(nc, in_maps, core_ids=list(range(B)))
    out = np.stack([np.asarray(r["y"], dtype=np.float32) for r in res.results], axis=0)
    return out
```

```python
import math
from contextlib import ExitStack
import numpy as np
import concourse.bass as bass
import concourse.mybir as mybir
from concourse.bass_utils import run_bass_kernel_spmd

F32 = mybir.dt.float32
BF16 = mybir.dt.bfloat16
ALU = mybir.AluOpType
AF = mybir.ActivationFunctionType

D = 1024
D_IN = 5920
D_FF = 2816
N_MEM = 256
EPS = 1e-6
SEM_LIMIT = 30000
MARKS = []
NLANES = 12
C_A = 8.0
C_M = math.sqrt(128.0)
O_QA, O_KA, O_VA, O_QB, O_KB, O_VB, O_GB, O_LR, O_QM, O_GL = 0, 512, 640, 768, 1024, 1280, 1792, 2304, 2336, 2848


class Buf:
    def __init__(self, name):
        self.name = name
        self.w = []
        self.r = {}


class T:
    def __init__(self, h, name):
        self.h = h
        self.b = Buf(name)

    def __getitem__(self, k):
        return self.h[k]


class Eng:
    def __init__(self, kern, name, h):
        self.k = kern
        self.name = name
        self.h = h
        self.seen = {}
        self.nsem = 0
        self.newsem()
        self.lanes = None
        self.li = 0

    def newsem(self):
        self.sem = self.k.new_sem(f"e_{self.name}_{self.nsem}")
        self.nsem += 1
        self.cnt = 0

    def wait(self, tok):
        sem, val, _ = tok
        if self.seen.get(sem, 0) >= val:
            return
        self.h.wait_ge(sem, val)
        self.seen[sem] = val


class Kern:
    def __init__(self, nc, es):
        self.nc = nc
        self.es = es
        self.nsems = 0
        self.pe = Eng(self, "pe", nc.tensor)
        self.act = Eng(self, "act", nc.scalar)
        self.dve = Eng(self, "dve", nc.vector)
        self.pool = Eng(self, "pool", nc.gpsimd)
        self.sp = Eng(self, "sp", nc.sync)
        for e in (self.sp, self.pool):
            e.lanes = [[self.new_sem(f"l_{e.name}_{i}"), 0] for i in range(NLANES)]

    def new_sem(self, name):
        self.nsems += 1
        return self.es.enter_context(self.nc.semaphore(f"{name}_{self.nsems}"))

    def _deps(self, eng, reads, writes, join=False):
        for t in reads:
            for tok in t.b.w:
                eng.wait(tok)
        for t in writes:
            b = t.b
            for tok in b.w:
                if eng.name == "pe" and tok[2] == "pe":
                    continue
                if join and tok[2].startswith("dma:"):
                    continue
                eng.wait(tok)
            for tok in b.r.values():
                if not (tok[2] == eng.name and eng.name == "pe"):
                    eng.wait(tok)

    def _update(self, key, tok, reads, writes, join=False):
        for t in writes:
            if join:
                t.b.w = [w_ for w_ in t.b.w if w_[2].startswith("dma:")] + [tok]
            else:
                t.b.w = [tok]
            t.b.r = {}
        for t in reads:
            if tok not in t.b.w:
                t.b.r[key] = tok

    def op(self, eng, fn, reads=(), writes=(), inc=True):
        self._deps(eng, reads, writes)
        if not inc:
            fn()
            return None
        ins = fn()
        eng.cnt += 1
        eng.total = getattr(eng, 'total', 0) + 1
        ins.then_inc(eng.sem, 1)
        tok = (eng.sem, eng.cnt, eng.name)
        self._update(eng.name, tok, reads, writes)
        if eng.cnt >= SEM_LIMIT:
            eng.newsem()
        return tok

    def dma(self, eng, out, in_, reads=(), writes=(), join=False):
        self._deps(eng, reads, writes, join)
        li = eng.li % NLANES
        eng.li += 1
        lane = eng.lanes[li]
        if lane[1] >= SEM_LIMIT:
            eng.wait((lane[0], lane[1], "x"))
            lane[0] = self.new_sem(f"l_{eng.name}_{li}")
            lane[1] = 0
        if lane[1] > 0:
            eng.wait((lane[0], lane[1], "x"))
        ins = eng.h.dma_start(out=out, in_=in_)
        lane[1] += 16
        ins.then_inc(lane[0], 16)
        key = f"dma:{eng.name}:{li}"
        tok = (lane[0], lane[1], key)
        self._update(key, tok, reads, writes, join)
        return tok

    def finish(self):
        for e in (self.sp, self.pool):
            for lane in e.lanes:
                if lane[1] > 0:
                    e.wait((lane[0], lane[1], "x"))


def build(S, depth=2, TG=512):
    NT = S // 128
    G = S // TG
    NTG = TG // 128
    nc = bass.Bass("TRN2", target_bir_lowering=False)
    es = ExitStack()
    es.enter_context(nc.allow_low_precision("bf16 matmul operands, fp32 accumulation"))
    es.enter_context(nc.allow_non_contiguous_dma("small strided param loads"))
    K = Kern(nc, es)
    pe, act, dve, pool, sp = K.pe, K.act, K.dve, K.pool, K.sp

    def din(name, shape, dt=F32):
        return T(nc.dram_tensor(name, list(shape), dt, kind="ExternalInput").ap(), name)

    def dscr(name, shape, dt):
        return T(nc.dram_tensor(name, list(shape), dt).ap(), name)

    x_in = din("x", [S, D])
    mem_in = din("mem", [N_MEM, D])
    biasg = din("biasg", [128, 6, 512])
    p_nmix = din("norm_mix_g", [2, D]); p_nffn = din("norm_ffn_g", [2, D]); p_nmem = din("norm_mem_g", [2, D])
    w_in = din("w_in", [2, D, D_IN])
    p_qna = din("q_norm_a", [2, 64]); p_kna = din("k_norm_a", [2, 64]); p_sink = din("sink_a", [2, 8])
    p_wup = din("w_decay_up", [2, 2, 16, 256]); p_bdec = din("b_decay", [2, 2, 256]); p_gng = din("gla_norm_g", [2, 128])
    w_kv = din("w_mem_kv", [2, D, D])
    p_qnm = din("q_norm_m", [2, 128]); p_knm = din("k_norm_m", [2, 128])
    w_br = din("w_branch", [2, 3, 512, D]); w_out = din("w_out", [2, D, D])
    w_fi = din("w_ffn_in", [2, D, 2 * D_FF]); w_fo = din("w_ffn_out", [2, D_FF, D])
    y_out = T(nc.dram_tensor("y", [S, D], F32, kind="ExternalOutput").ap(), "y")

    T_QA, T_QBKB, T_VB, T_GB, T_QM, T_P1, T_KV0, T_KV1, T_OUT0, T_OUT1, T_MERGE, T_FFI, NT_W = 0, 1, 2, 3, 4, 5, 6, 7, 8, 9, 10, 18, 29
    WT_h = nc.dram_tensor("WT", [2, NT_W, 128, 8, 512], BF16).ap()
    WFO_h = nc.dram_tensor("WFO", [2, 4, 128, 11, 512], BF16).ap()
    WMA_h = nc.dram_tensor("WMA", [2, 8, 128, 4, 128], BF16).ap()
    WL_h = nc.dram_tensor("WL", [2, 128, 8, 32], BF16).ap()
    wt = [[T(WT_h[l, t], f"WT{l}_{t}") for t in range(NT_W)] for l in range(2)]
    wfod = [[T(WFO_h[l, i], f"WFO{l}_{i}") for i in range(4)] for l in range(2)]
    wmad = [[T(WMA_h[l, c], f"WMA{l}_{c}") for c in range(8)] for l in range(2)]
    wld = [T(WL_h[l], f"WL{l}") for l in range(2)]
    xbuf = dscr("xbuf", [S, D], F32)
    hTd = dscr("hTd", [128, 8, S], BF16)
    kaTd = dscr("kaTd", [64, 2, S], BF16)
    vad = dscr("vad", [128, NT, 128], BF16)
    Sbd = dscr("Sbd", [64, NT, 512], BF16)

    def sb(name, shape, dt):
        return T(es.enter_context(nc.sbuf_tensor(name, list(shape), dt)), name)

    def psum(name, shape, dt):
        return T(es.enter_context(nc.psum_tensor(name, list(shape), dt)), name)

    PS = [psum(f"ps{i}", [128, 512], F32) for i in range(7)]
    PST = psum("pst", [128, 1024], BF16)
    psi = [0]

    def ps():
        p = PS[psi[0] % len(PS)]
        psi[0] += 1
        return p

    def mm(out, lhsT, rhs, start, stop, reads, writes, inc=True):
        K.op(pe, lambda: nc.tensor.matmul(out, lhsT, rhs, start=start, stop=stop), reads, writes, inc=inc)

    ident = sb("ident", [128, 128], BF16)
    ones_bf = sb("ones_bf", [128, 128], BF16)
    ones32 = sb("ones32", [128, 128], F32)
    tril = sb("tril", [128, 128], F32)
    triu = sb("triu", [128, 128], F32)
    mask_f = sb("mask_f", [128, 128], F32)
    mask_b = sb("mask_b", [128, 128], F32)
    biasT = sb("biasT", [128, 6, 512], F32)
    K.op(pool, lambda: nc.gpsimd.memset(ones32[:], 1.0), [], [ones32])
    K.op(pool, lambda: nc.gpsimd.memset(ones_bf[:], 1.0), [], [ones_bf])
    K.op(pool, lambda: nc.gpsimd.affine_select(ident[:], ones_bf[:], [[-1, 128]], ALU.is_equal, 0.0, base=0, channel_multiplier=1), [ones_bf], [ident])
    K.op(pool, lambda: nc.gpsimd.affine_select(tril[:], ones32[:], [[1, 128]], ALU.is_ge, 0.0, base=0, channel_multiplier=-1), [ones32], [tril])
    K.op(pool, lambda: nc.gpsimd.affine_select(triu[:], ones32[:], [[-1, 128]], ALU.is_ge, 0.0, base=0, channel_multiplier=1), [ones32], [triu])
    K.op(pool, lambda: nc.gpsimd.memset(mask_f[:], 1.0), [], [mask_f])
    K.op(pool, lambda: nc.gpsimd.memset(mask_b[:], 1.0), [], [mask_b])
    K.op(pool, lambda: nc.gpsimd.affine_select(mask_f[:], mask_f[:], [[1, 128]], ALU.is_ge, 0.0, base=0, channel_multiplier=-1), [mask_f], [mask_f])
    K.op(pool, lambda: nc.gpsimd.affine_select(mask_b[:], mask_b[:], [[-1, 128]], ALU.is_ge, 0.0, base=-1, channel_multiplier=1), [mask_b], [mask_b])
    K.dma(sp, biasT[:], biasg[:], [biasg], [biasT])
    for kvh in range(2):
        v0 = biasT[:, kvh * 3 + 0, :].rearrange("p (h c) -> p h c", h=4)
        K.op(pool, lambda v0=v0: nc.gpsimd.affine_select(v0, v0, [[0, 4], [-1, 128]], ALU.is_ge, -1e30, base=0, channel_multiplier=1), [biasT], [biasT])
        v2 = biasT[:, kvh * 3 + 2, :].rearrange("p (h c) -> p h c", h=4)
        K.op(pool, lambda v2=v2: nc.gpsimd.affine_select(v2, v2, [[0, 4], [1, 128]], ALU.is_ge, -1e30, base=0, channel_multiplier=-1), [biasT], [biasT])

    def cast_jobs(l):
        J = []

        def rows(src, c0, n):
            return src[l, :, c0:c0 + n].rearrange("(k p) c -> p k c", p=128)

        def add(t, d0, src, c0, n):
            J.append((wt[l][t], wt[l][t][:, :, d0:d0 + n], src, rows(src, c0, n)))
        add(T_KV0, 0, w_kv, 0, 512); add(T_KV1, 0, w_kv, 512, 512)
        add(T_P1, 0, w_in, O_KA, 256); add(T_P1, 256, w_in, O_KB, 256)
        add(T_VB, 0, w_in, O_VB, 512)
        J.append((wld[l], wld[l][:, :, :], w_in, rows(w_in, O_LR, 32)))
        add(T_QA, 0, w_in, O_QA, 512); add(T_QBKB, 0, w_in, O_QB, 512); add(T_GB, 0, w_in, O_GB, 512); add(T_QM, 0, w_in, O_QM, 512)
        for c in range(8):
            cs = slice(c * 128, (c + 1) * 128)
            for kvh in range(2):
                J.append((wmad[l][c], wmad[l][c][kvh * 64:(kvh + 1) * 64, :, :], w_br,
                          w_br[l, 0, kvh * 256:(kvh + 1) * 256, cs].rearrange("(h p) c -> p h c", p=64)))
            for j in range(3):
                add(T_MERGE + c, j * 128, w_in, O_GL + j * D + c * 128, 128)
            for j in (1, 2):
                J.append((wt[l][T_MERGE + c], wt[l][T_MERGE + c][:, 4 * (j - 1):4 * j, 384:512], w_br,
                          w_br[l, j, :, cs].rearrange("(h p) c -> p h c", p=128)))
        add(T_OUT0, 0, w_out, 0, 512); add(T_OUT1, 0, w_out, 512, 512)
        for fc in range(11):
            for u in range(2):
                add(T_FFI + fc, u * 256, w_fi, u * D_FF + fc * 256, 256)
        for half in range(2):
            for kh in range(2):
                J.append((wfod[l][half * 2 + kh], wfod[l][half * 2 + kh][:, :, :], w_fo,
                          w_fo[l, kh * 1408:(kh + 1) * 1408, half * 512:(half + 1) * 512].rearrange("(k p) c -> p k c", p=128)))
        return J

    def issue_casts(jobs, n):
        for _ in range(min(n, len(jobs))):
            dst_t, dst_ap, src_t, src_ap = jobs.pop(0)
            K.dma(pool, dst_ap, src_ap, [src_t], [dst_t], join=True)

    wup = sb("wup", [16, 2, 256], BF16)
    jobs0 = cast_jobs(0)
    K.dma(pool, wup[:], p_wup[0].rearrange("d r c -> r d c"), [p_wup], [wup])
    issue_casts(jobs0, len(jobs0))
    jobs1 = cast_jobs(1) if depth > 1 else []
    per_group = -(-len(jobs1) // max(1, G - 1)) if G > 1 else len(jobs1)

    gmix = sb("gmix", [128, D], F32); gffn = sb("gffn", [128, D], F32); gmem = gffn
    bdec = sb("bdec", [128, 512], F32)
    qna = sb("qna", [64, 1], F32); kna = sb("kna", [64, 1], F32)
    qnm = sb("qnm", [128, 1], F32); knm = sb("knm", [128, 1], F32); gng = sb("gng", [128, 1], F32)
    sinkraw = sb("sinkraw", [64, 8], F32); sinkexp = sb("sinkexp", [64, 8], F32); sinkrow = sb("sinkrow", [1, 2, 512], BF16)
    kmT = sb("kmT", [128, 4, 256], BF16); vm = sb("vm", [128, 2, 512], BF16)
    xg = sb("xg", [128, NTG, D], F32)
    hT = sb("hT", [128, 8, TG], BF16)
    hbfs = [sb(f"hbf{i}", [128, D], BF16) for i in range(4)]
    st1s = [sb(f"st1_{i}", [128, 1], F32) for i in range(4)]
    st2s = [sb(f"st2_{i}", [128, 1], F32) for i in range(4)]
    wA = [sb(f"wA{i}", [128, 8, 512], BF16) for i in range(3)]
    wai = [0]
    wl = sb("wl", [128, 8, 32], BF16)
    def view(ap, name):
        return T(ap, name)

    assert TG == 512
    arena1 = es.enter_context(nc.sbuf_tensor("arena1", [128, 11264], BF16))
    actT = view(arena1[:, :].rearrange("p (f c) -> p f c", f=22), "actT")
    qaT = view(arena1[0:64, 0:4096].rearrange("p (h c) -> p h c", h=8), "qaT")
    qbT32 = view(arena1[0:64, 4096:6144].rearrange("p (h c) -> p h c", h=4), "qbT")
    kbT32 = view(arena1[0:64, 6144:8192].rearrange("p (h c) -> p h c", h=4), "kbT")
    sog = view(arena1[:, 8192:10240].rearrange("p (h c) -> p h c", h=4), "sog")
    lrT = view(arena1[0:16, 10240:11264].rearrange("p (d c) -> p d c", d=2), "lrT")
    A1V = [qaT, qbT32, kbT32, sog, lrT]
    arena2 = es.enter_context(nc.sbuf_tensor("arena2", [128, 6144], BF16))
    arena3 = es.enter_context(nc.sbuf_tensor("arena3", [128, 6144], BF16))
    wfo = [view(arena2[:, 0:5632].rearrange("p (f c) -> p f c", f=11), "wfo0"),
           view(arena3[:, 0:5632].rearrange("p (f c) -> p f c", f=11), "wfo1")]
    oaT = view(arena2[:, 0:2048].rearrange("p (h c) -> p h c", h=4), "oaT")
    omT = view(arena2[:, 4096:6144].rearrange("p (h c) -> p h c", h=4), "omT")
    mergedT = view(arena3[:, 0:4096].rearrange("p (k c) -> p k c", k=8), "mergedT")
    obT = view(arena3[:, 4096:6144].rearrange("p (h c) -> p h c", h=4), "obT")

    def handoff(srcs, dsts):
        for d_ in dsts:
            for s_ in srcs:
                if s_ is d_:
                    continue
                for i_, w_ in enumerate(s_.b.w):
                    d_.b.r["h:" + s_.b.name + ":w%d" % i_] = w_
                for k_, tok_ in list(s_.b.r.items()):
                    d_.b.r["h:" + s_.b.name + ":" + k_] = tok_

    kbtm = sb("kbtm", [128, NTG, 256], F32)
    vbtm = sb("vbtm", [128, NTG, 512], BF16)
    qmT = sb("qmT", [128, 4, TG], BF16)
    kaT = sb("kaT", [64, 2, TG + 256], BF16); va = sb("va", [128, NTG + 2, 128], BF16)
    obT32 = sb("obT32", [128, 4, TG], F32)
    f32a = sb("f32a", [128, 512], F32); f32b = sb("f32b", [128, 512], F32); f32c = sb("f32c", [128, 512], F32)
    f32d = sb("f32d", [128, 512], F32)
    sqb = sb("sqb", [128, 512], BF16)
    sqb2 = sb("sqb2", [128, 512], BF16)
    pT = [sb(f"pT{i}", [128, 512], BF16) for i in range(3)]
    nlas = [sb(f"nla{i}", [128, 512], F32) for i in range(2)]
    Eq = [sb(f"Eq{d}", [64, 512], F32) for d in range(2)]
    Ek0 = sb("Ek", [64, 512], F32)
    Ek = [Ek0, Ek0]
    qds = [[sb(f"qd{i}_{d}", [64, 512], BF16) for d in range(2)] for i in range(2)]
    kds = [[sb(f"kd{i}_{d}", [64, 512], BF16) for d in range(2)] for i in range(2)]
    Etm = sb("Etm", [128, 256], F32)
    kdtms = [sb(f"kdtm{i}", [128, 256], BF16) for i in range(2)]
    Ams = [[sb(f"Am{i}_{d}", [128, 512], BF16) for d in range(2)] for i in range(2)]
    decs = [sb(f"dec{i}", [64, 4], F32) for i in range(2)]
    Rst = sb("Rst", [64, 512], F32); Sf = sb("Sf", [64, 512], F32)
    Sfbs = [sb(f"Sfb{i}", [64, 512], BF16) for i in range(2)]
    dprev = sb("dprev", [64, 4], F32)
    Sblc = [sb(f"Sblc{i}", [64, 512], BF16) for i in range(2)]
    wbA = [sb(f"wbA{i}", [128, 4, 128], BF16) for i in range(2)]
    mnT = hT
    print("SBUF bytes remaining per partition:", nc.sbuf_bytes_remaining)

    def wnext():
        w = wA[wai[0] % 3]
        wai[0] += 1
        return w

    def wview(wsc, l, c0, ncols, rows=D):
        return wsc[l, 0:rows, c0:c0 + ncols].rearrange("(k p) c -> p k c", p=128)

    def rsqrt_col(dst, ss, n, parts=128):
        K.op(act, lambda: nc.scalar.activation(dst[0:parts, :], ss[0:parts, :], AF.Sqrt, bias=EPS, scale=1.0 / n), [ss], [dst])
        K.op(dve, lambda: nc.vector.reciprocal(dst[0:parts, :], dst[0:parts, :]), [dst], [dst])

    def norm_chain(src_t, src_ap, gt, i):
        hb, s1_, s2_ = hbfs[i], st1s[i], st2s[i]
        K.op(act, lambda: nc.scalar.activation(hb[:], src_ap, AF.Square, accum_out=s1_[:]), [src_t], [hb, s1_])
        rsqrt_col(s2_, s1_, D)
        K.op(dve, lambda: nc.vector.scalar_tensor_tensor(hb[:], src_ap, s2_[:], gt[:], ALU.mult, ALU.mult), [src_t, s2_, gt], [hb])

    def norm_tr(i, dstT, col0):
        hb = hbfs[i]
        for k in range(8):
            K.op(pe, lambda k=k: nc.tensor.transpose(PST[:, k * 128:(k + 1) * 128], hb[:, k * 128:(k + 1) * 128], ident[:]), [hb, ident], [PST])
        K.op(act, lambda: nc.scalar.copy(dstT[:, :, col0:col0 + 128], PST[:].rearrange("p (k c) -> p k c", k=8)), [PST], [dstT])

    def norm_transpose(src_t, src_ap, gt, dstT, col0, i=0):
        norm_chain(src_t, src_ap, gt, i)
        norm_tr(i, dstT, col0)

    def qknorm_p1(p, parts, ntok):
        K.op(act, lambda: nc.scalar.activation(sqb[0:parts, 0:ntok], p[0:parts, 0:ntok], AF.Square), [p], [sqb])

    def qknorm_p2(p, parts, ntok, gcol, dst_ap, dst_t, n):
        p2 = ps()
        mm(p2[0:parts, 0:ntok], ones_bf[0:parts, 0:parts], sqb[0:parts, 0:ntok], True, True, [ones_bf, sqb], [p2])
        K.op(act, lambda: nc.scalar.activation(f32c[0:parts, 0:ntok], p2[0:parts, 0:ntok], AF.Ln, bias=EPS, scale=1.0 / n), [p2], [f32c])
        K.op(act, lambda: nc.scalar.activation(f32c[0:parts, 0:ntok], f32c[0:parts, 0:ntok], AF.Exp, scale=-0.5), [f32c], [f32c])
        K.op(dve, lambda: nc.vector.scalar_tensor_tensor(dst_ap, p[0:parts, 0:ntok], gcol[0:parts, :], f32c[0:parts, 0:ntok], ALU.mult, ALU.mult), [p, gcol, f32c], [dst_t])

    def qknorm_fm(p, parts, ntok, gcol, dst_ap, dst_t, n):
        qknorm_p1(p, parts, ntok)
        qknorm_p2(p, parts, ntok, gcol, dst_ap, dst_t, n)

    def proj_norm_pipelined(specs):
        pend = None
        for (pf, parts, ntok, gcol, dst_ap, dst_t, n) in specs:
            p = pf()
            if pend is not None:
                qknorm_p2(*pend)
            qknorm_p1(p, parts, ntok)
            pend = (p, parts, ntok, gcol, dst_ap, dst_t, n)
        if pend is not None:
            qknorm_p2(*pend)

    def proj_fm(w, c0, M, rhsT, ntok, nk=8):
        p = ps()
        for k in range(nk):
            mm(p[0:M, 0:ntok], w[:, k, c0:c0 + M], rhsT[:, k, 0:ntok], k == 0, k == nk - 1, [w, rhsT], [p], inc=(k == nk - 1))
        return p

    def proj_tm(w, c0, N, lhsT, t0, nk=8):
        p = ps()
        for k in range(nk):
            mm(p[:, 0:N], lhsT[:, k, t0:t0 + 128], w[:, k, c0:c0 + N], k == 0, k == nk - 1, [w, lhsT], [p], inc=(k == nk - 1))
        return p

    f32d_h = [T(f32d.h[:, 0:256], "f32d_lo"), T(f32d.h[:, 256:512], "f32d_hi")]

    def gla_logits(t, d, lrrow, nla, bank=None):
        tmp = f32d_h[d]
        p = bank if bank is not None else ps()
        pc = slice(d * 256, (d + 1) * 256) if bank is not None else slice(0, 256)
        mm(p[:, pc], lrT[0:16, lrrow, t * 128:(t + 1) * 128], wup[0:16, d, :], True, True, [lrT, wup], [p])
        yield
        K.op(dve, lambda: nc.vector.tensor_tensor(tmp[:, :], p[:, pc], bdec[:, d * 256:(d + 1) * 256], ALU.add), [p, bdec], [tmp])
        yield
        K.op(act, lambda: nc.scalar.activation(tmp[:, :], tmp[:, :], AF.Exp, scale=-1.0), [tmp], [tmp])
        yield
        K.op(act, lambda: nc.scalar.activation(nla[:, d * 256:(d + 1) * 256], tmp[:, :], AF.Ln, bias=1.0), [tmp], [nla])
        yield

    def gla_tm_kd(t, d, tri_t, nla, kdtm, bank=None):
        p = bank if bank is not None else ps()
        mm(p[:, 0:256], tri_t[:, :], nla[:, d * 256:(d + 1) * 256], True, True, [tri_t, nla], [p])
        yield
        K.op(act, lambda: nc.scalar.activation(Etm[:], p[:, 0:256], AF.Exp, scale=1.0 / 16.0), [p], [Etm])
        yield
        K.op(dve, lambda: nc.vector.tensor_tensor(kdtm[:], kbtm[:, t, :], Etm[:], ALU.mult), [kbtm, Etm], [kdtm])
        yield

    def gla_u0(t, kdtm, bank=None):
        pu = bank if bank is not None else ps()
        for h in range(4):
            mm(pu[0:64, h * 128:(h + 1) * 128], kdtm[:, h * 64:(h + 1) * 64], vbtm[:, t, h * 128:(h + 1) * 128], True, True, [kdtm, vbtm], [pu])
        return pu

    def bc4(ap):
        return ap.unsqueeze(2).broadcast_to([64, 4, 128])

    def v3(ap):
        return ap.rearrange("p (h c) -> p h c", h=4)

    def state_enter(Sfb):
        K.op(dve, lambda: nc.vector.tensor_tensor(v3(Sf[:]), v3(Rst[:]), bc4(dprev[:]), ALU.mult), [Rst, dprev], [Sf])
        K.op(act, lambda: nc.scalar.copy(Sfb[:], Sf[:]), [Sf], [Sfb])

    def state_leave(pu, dec):
        K.op(dve, lambda: nc.vector.tensor_tensor(Rst[:], pu[0:64, :], Sf[:], ALU.add), [pu, Sf], [Rst])
        K.op(dve, lambda: nc.vector.tensor_copy(dprev[:], dec[:]), [dec], [dprev])

    def staged_gen(order, stages):
        n, ns = len(order), len(stages)
        for step in range(n + ns - 1):
            gens = []
            for k, st in enumerate(stages):
                i = step - k
                if 0 <= i < n:
                    gens.append(st(order[i]))
            while gens:
                for g_ in list(gens):
                    try:
                        next(g_)
                    except StopIteration:
                        gens.remove(g_)
                yield

    def staged(order, stages):
        for _ in staged_gen(order, stages):
            pass

    def interleave(gens):
        gens = list(gens)
        while gens:
            for g_ in list(gens):
                try:
                    next(g_)
                except StopIteration:
                    gens.remove(g_)

    vaug = [view(hbfs[2 + i].h[:, 0:(NTG + 2) * 128].rearrange("p (t c) -> p t c", c=128), f"vaug{i}") for i in range(2)]
    sinksel = sb("sinksel", [1, 128], BF16)
    K.op(dve, lambda: nc.vector.memset(sinksel[:], 0.0), [], [sinksel])
    K.op(dve, lambda: nc.vector.memset(sinksel[0:1, 64:128], 1.0), [], [sinksel])
    xg2 = view(arena1[:, 0:8192].bitcast(F32).rearrange("p (t c) -> p t c", t=NTG), "xg2")
    hT2 = view(arena2[:, 0:4096].rearrange("p (k c) -> p k c", k=8), "hT2")
    P1V = [xg2, hT2, lrT]
    ARV = A1V + [actT, oaT, omT, mergedT, obT, wfo[0], wfo[1]]

    def gsl(g):
        return slice(g * TG, (g + 1) * TG)

    def mark(label):
        MARKS.append((label, getattr(pe, 'total', 0)))

    for l in range(depth):
        if l == 1:
            issue_casts(jobs1, len(jobs1))
        xsrc = x_in if l == 0 else xbuf
        xdst = xbuf if l < depth - 1 else y_out
        K.dma(sp, gmix[:], p_nmix[l:l + 1, :].partition_broadcast(128), [p_nmix], [gmix])
        K.dma(sp, gmem[:], p_nmem[l:l + 1, :].partition_broadcast(128), [p_nmem], [gmem])
        K.dma(sp, bdec[:], p_bdec[l:l + 1, :, :].rearrange("o d c -> o (d c)").partition_broadcast(128), [p_bdec], [bdec])
        if l > 0:
            K.dma(pool, wup[:], p_wup[l].rearrange("d r c -> r d c"), [p_wup], [wup])
        K.dma(sp, qna[:], p_qna[l:l + 1, :].rearrange("o c -> c o"), [p_qna], [qna])
        K.dma(sp, kna[:], p_kna[l:l + 1, :].rearrange("o c -> c o"), [p_kna], [kna])
        K.dma(sp, qnm[:], p_qnm[l:l + 1, :].rearrange("o c -> c o"), [p_qnm], [qnm])
        K.dma(sp, knm[:], p_knm[l:l + 1, :].rearrange("o c -> c o"), [p_knm], [knm])
        K.dma(sp, gng[:], p_gng[l:l + 1, :].rearrange("o c -> c o"), [p_gng], [gng])
        K.dma(sp, sinkraw[:], p_sink[l:l + 1, :].partition_broadcast(64), [p_sink], [sinkraw])
        K.op(act, lambda: nc.scalar.activation(sinkexp[:], sinkraw[:], AF.Exp, bias=-C_A), [sinkraw], [sinkexp])
        for kvh_ in range(2):
            K.op(dve, lambda kvh_=kvh_: nc.vector.tensor_copy(
                sinkrow[0:1, kvh_, :].rearrange("p (h c) -> p h c", h=4),
                sinkexp[0:1, kvh_ * 4:(kvh_ + 1) * 4].unsqueeze(2).broadcast_to([1, 4, 128])), [sinkexp], [sinkrow])

        mark(f'L{l} mem')
        K.dma(sp, xg[:, 0:2, :], mem_in[:, :].rearrange("(t p) c -> p t c", p=128), [mem_in], [xg])
        for t in range(2):
            norm_transpose(xg, xg[:, t, :], gmem, mnT, t * 128, i=t)
        for half in range(2):
            w = wnext()
            K.dma(sp, w[:], wt[l][T_KV0 + half][:, :, :], [wt[l][T_KV0 + half]], [w])
            if half == 0:
                for h in range(4):
                    p = proj_fm(w, h * 128, 128, mnT, 256)
                    qknorm_fm(p, 128, 256, knm, kmT[:, h, :], kmT, 128)
            else:
                for t in range(2):
                    p = proj_tm(w, 0, 512, mnT, t * 128)
                    K.op(act, lambda p=p, t=t: nc.scalar.copy(vm[:, t, :], p[:, :]), [p], [vm])
        K.dma(sp, gffn[:], p_nffn[l:l + 1, :].partition_broadcast(128), [p_nffn], [gffn])
        K.dma(sp, wl[:], wld[l][:, :, :], [wld[l]], [wl])

        mark(f'L{l} P1')
        handoff(ARV, P1V); handoff([f32d], f32d_h)
        wp1 = wnext()
        K.dma(sp, wp1[:], wt[l][T_P1][:, :, :], [wt[l][T_P1]], [wp1])
        wp1v = wnext()
        K.dma(sp, wp1v[:], wt[l][T_VB][:, :, :], [wt[l][T_VB]], [wp1v])
        K.op(dve, lambda: nc.vector.memset(Rst[:], 0.0), [], [Rst])
        K.op(dve, lambda: nc.vector.memset(dprev[:], 1.0), [], [dprev])
        XGs = [xg, xg2]
        HTs = [hT, hT2]

        def p1_xload(g):
            xt = XGs[g % 2]
            K.dma(sp, xt[:], xsrc[gsl(g), :].rearrange("(t p) c -> p t c", p=128), [xsrc], [xt])

        def p1_A(g):
            xt, ht = XGs[g % 2], HTs[g % 2]
            for t in range(NTG):
                norm_chain(xt, xt[:, t, :], gmix, t)
                yield
            for t in range(NTG):
                norm_tr(t, ht, t * 128)
                yield
            K.dma(sp, hTd[:, :, gsl(g)], ht[:], [ht], [hTd])
            yield

        def p1_B(g):
            ht = HTs[g % 2]
            for kvh in range(2):
                p = proj_fm(wp1, kvh * 64, 64, ht, TG)
                yield
                qknorm_fm(p, 64, TG, kna, kaT[0:64, kvh, 0:TG], kaT, 64)
                yield
            K.dma(sp, kaTd[:, :, gsl(g)], kaT[:, :, 0:TG], [kaT], [kaTd])
            for t in range(NTG):
                p = proj_tm(wp1, 128, 128, ht, t * 128)
                K.op(act, lambda p=p, t=t: nc.scalar.copy(va[:, t, :], p[:, 0:128]), [p], [va])
                yield
            K.dma(sp, vad[:, g * NTG:(g + 1) * NTG, :], va[:, 0:NTG, :], [va], [vad])
            p = proj_fm(wl, 16, 16, ht, TG)
            K.op(act, lambda p=p: nc.scalar.copy(lrT[0:16, 1, :], p[0:16, 0:TG]), [p], [lrT])
            yield

            def p1_s1(t):
                p = proj_tm(wp1, 256, 256, ht, t * 128)
                K.op(act, lambda: nc.scalar.copy(kbtm[:, t, :], p[:, 0:256]), [p], [kbtm])
                yield
                p2_ = proj_tm(wp1v, 0, 512, ht, t * 128)
                K.op(act, lambda: nc.scalar.copy(vbtm[:, t, :], p2_[:, :]), [p2_], [vbtm])
                yield
                yield from gla_logits(t, 1, 1, nlas[t % 2])

            def p1_s2(t):
                nla = nlas[t % 2]
                yield from gla_tm_kd(t, 1, triu, nla, kdtms[t % 2])
                pd = ps()
                for h in range(4):
                    mm(pd[0:64, h:h + 1], nla[:, 256 + h * 64:256 + (h + 1) * 64], ones32[:, 0:1], True, True, [nla, ones32], [pd])
                K.op(act, lambda: nc.scalar.activation(decs[t % 2][:], pd[0:64, 0:4], AF.Exp, scale=-1.0 / 16.0), [pd], [decs[t % 2]])
                yield

            def p1_s3(t):
                n = g * NTG + t
                Sfb = Sfbs[t % 2]
                state_enter(Sfb)
                yield
                K.dma(sp, Sbd[:, n, :], Sfb[:], [Sfb], [Sbd])
                pu = gla_u0(t, kdtms[t % 2])
                yield
                state_leave(pu, decs[t % 2])
                yield
            yield from staged_gen(list(range(NTG - 1, -1, -1)), [p1_s1, p1_s2, p1_s3])

        gs_ = list(range(G - 1, -1, -1))
        p1_xload(gs_[0])
        if G > 1:
            p1_xload(gs_[1])
        interleave([p1_A(gs_[0])])
        for i_ in range(1, G):
            if i_ + 1 < G:
                p1_xload(gs_[i_ + 1])
            interleave([p1_B(gs_[i_ - 1]), p1_A(gs_[i_])])
        interleave([p1_B(gs_[-1])])
        handoff(P1V, ARV)

        mark(f'L{l} P2')
        K.op(dve, lambda: nc.vector.memset(Rst[:], 0.0), [], [Rst])
        K.op(dve, lambda: nc.vector.memset(dprev[:], 1.0), [], [dprev])
        for g in range(G):
            K.dma(sp, hT[:], hTd[:, :, gsl(g)], [hTd], [hT])
            lo = max(0, g * TG - 128); hi = min(S, g * TG + TG + 128)
            off = lo - (g * TG - 128)
            K.dma(sp, kaT[:, :, off:off + (hi - lo)], kaTd[:, :, lo:hi], [kaTd], [kaT])
            handoff([actT], A1V); handoff([wfo[0]], [oaT, omT]); handoff([wfo[1]], [mergedT, obT])

            if l == 0:
                issue_casts(jobs1, per_group)
            mark(f'L{l} g{g} inproj')
            w = wnext(); K.dma(sp, w[:], wt[l][T_QA][:, :, :], [wt[l][T_QA]], [w])
            proj_norm_pipelined([((lambda h=h, w=w: proj_fm(w, h * 64, 64, hT, TG)), 64, TG, qna, qaT[0:64, h, :], qaT, 64) for h in range(8)])
            w = wnext(); K.dma(sp, w[:], wt[l][T_QBKB][:, :, :], [wt[l][T_QBKB]], [w])
            for h in range(4):
                p = proj_fm(w, h * 64, 64, hT, TG)
                K.op(act, lambda p=p, h=h: nc.scalar.copy(qbT32[:, h, :], p[0:64, 0:TG]), [p], [qbT32])
            for h in range(4):
                p = proj_fm(w, 256 + h * 64, 64, hT, TG)
                K.op(act, lambda p=p, h=h: nc.scalar.copy(kbT32[:, h, :], p[0:64, 0:TG]), [p], [kbT32])
            for t in range(NTG):
                p = proj_tm(w, 256, 256, hT, t * 128)
                K.op(act, lambda p=p, t=t: nc.scalar.copy(kbtm[:, t, :], p[:, 0:256]), [p], [kbtm])
            w = wnext(); K.dma(sp, w[:], wt[l][T_VB][:, :, :], [wt[l][T_VB]], [w])
            for t in range(NTG):
                p = proj_tm(w, 0, 512, hT, t * 128)
                K.op(act, lambda p=p, t=t: nc.scalar.copy(vbtm[:, t, :], p[:, :]), [p], [vbtm])
            w = wnext(); K.dma(sp, w[:], wt[l][T_GB][:, :, :], [wt[l][T_GB]], [w])
            for c in range(4):
                p = proj_fm(w, c * 128, 128, hT, TG)
                K.op(act, lambda p=p, c=c: nc.scalar.activation(sog[:, c, :], p[:, 0:TG], AF.Silu), [p], [sog])
            for d in range(2):
                p = proj_fm(wl, d * 16, 16, hT, TG)
                K.op(act, lambda p=p, d=d: nc.scalar.copy(lrT[0:16, d, :], p[0:16, 0:TG]), [p], [lrT])
            w = wnext(); K.dma(sp, w[:], wt[l][T_QM][:, :, :], [wt[l][T_QM]], [w])
            proj_norm_pipelined([((lambda h=h, w=w: proj_fm(w, h * 128, 128, hT, TG)), 128, TG, qnm, qmT[:, h, :], qmT, 128) for h in range(4)])

            handoff([hbfs[2], hbfs[3]], vaug)
            for kvh_ in range(2):
                K.op(dve, lambda kvh_=kvh_: nc.vector.memset(vaug[kvh_][:, :, 64:128], 1.0), [], [vaug[kvh_]])
                K.dma(sp, vaug[kvh_][:, off // 128:off // 128 + (hi - lo) // 128, 0:64],
                      vad[:, lo // 128:hi // 128, kvh_ * 64:(kvh_ + 1) * 64], [vad], [vaug[kvh_]])
            mark(f'L{l} g{g} attnA')
            SC = [PS[0], PS[1]]
            ACC = [PS[2], PS[3]]
            GB = [PS[4], PS[5], PS[6]]
            fA = [f32a, f32c]
            items = []
            for t in range(NTG):
                n = g * NTG + t
                for kvh in range(2):
                    js = [j for j in range(3) if 0 <= n - 1 + j < NT]
                    for ji, j in enumerate(js):
                        items.append((t, kvh, j, ji, len(js)))

            def a_s1(i):
                t, kvh, j, ji, nj = items[i]
                pscr = SC[i % len(SC)]
                kc = (t + j) * 128
                mm(pscr[:, :], kaT[0:64, kvh, kc:kc + 128], qaT[0:64, kvh * 4:(kvh + 1) * 4, t * 128:(t + 1) * 128],
                   True, True, [kaT, qaT], [pscr])
                fa = fA[i % 2]
                K.op(dve, lambda: nc.vector.scalar_tensor_tensor(
                    fa[:], pscr[:, :], 0.125, biasT[:, kvh * 3 + j, :], ALU.mult, ALU.add), [pscr, biasT], [fa])
                pt = pT[i % 3]
                K.op(act, lambda: nc.scalar.activation(pt[:], fa[:], AF.Exp, bias=-C_A), [fa], [pt])

            def a_s2(i):
                t, kvh, j, ji, nj = items[i]
                po = ACC[(t * 2 + kvh) % 2]
                pt = pT[i % 3]
                mm(po[:, :], vaug[kvh][:, t + j, :], pt[:], ji == 0, False, [vaug[kvh], pt], [po])
                if ji == nj - 1:
                    mm(po[:, :], sinksel[0:1, :], sinkrow[0:1, kvh, :], False, True, [sinksel, sinkrow], [po])
                    K.op(act, lambda: nc.scalar.activation(f32b[0:64, :], po[64:128, :], AF.Ln), [po], [f32b])
                    K.op(act, lambda: nc.scalar.activation(f32b[0:64, :], f32b[0:64, :], AF.Exp, scale=-1.0), [f32b], [f32b])
                    K.op(dve, lambda: nc.vector.tensor_tensor(
                        oaT[kvh * 64:(kvh + 1) * 64, :, t * 128:(t + 1) * 128], v3(po[0:64, :]), v3(f32b[0:64, :]), ALU.mult), [po, f32b], [oaT])
            LAG = 1

            def attn_gen():
                for i in range(len(items) + LAG):
                    if i < len(items):
                        a_s1(i)
                        yield
                    if i >= LAG:
                        a_s2(i - LAG)
                        yield
                for i in range(len(mitems) + LAG):
                    if i < len(mitems):
                        m_s1(i)
                        yield
                    if i >= LAG:
                        m_s2(i - LAG)
                        yield

            mark(f'L{l} g{g} attnM')
            mitems = [(h, mc) for h in range(4) for mc in range(2)]

            def m_s1(i):
                h, mc = mitems[i]
                pscr = SC[i % len(SC)]
                mm(pscr[:, 0:TG], kmT[:, h, mc * 128:(mc + 1) * 128], qmT[:, h, :], True, True, [kmT, qmT], [pscr])
                pt = pT[i % 3]
                K.op(act, lambda: nc.scalar.activation(pt[:, 0:TG], pscr[:, 0:TG], AF.Exp, bias=-C_M, scale=1.0 / C_M), [pscr], [pt])

            def m_s2(i):
                h, mc = mitems[i]
                po, pden = ACC[0], ACC[1]
                pt = pT[i % 3]
                mm(po[:, 0:TG], vm[:, mc, h * 128:(h + 1) * 128], pt[:, 0:TG], mc == 0, mc == 1, [vm, pt], [po])
                mm(pden[:, 0:TG], ones_bf[:, :], pt[:, 0:TG], mc == 0, mc == 1, [ones_bf, pt], [pden])
                if mc == 1:
                    K.op(act, lambda: nc.scalar.activation(f32b[:, 0:TG], pden[:, 0:TG], AF.Ln), [pden], [f32b])
                    K.op(act, lambda: nc.scalar.activation(f32b[:, 0:TG], f32b[:, 0:TG], AF.Exp, scale=-1.0), [f32b], [f32b])
                    K.op(dve, lambda: nc.vector.tensor_tensor(omT[:, h, :], po[:, 0:TG], f32b[:, 0:TG], ALU.mult), [po, f32b], [omT])

            handoff([f32d], f32d_h)
            mark(f'L{l} g{g} gla')
            def g_s1(t):
                ga = gla_logits(t, 0, 0, nlas[t % 2], bank=GB[0])
                gb = gla_logits(t, 1, 1, nlas[t % 2], bank=GB[0])
                for _ in ga:
                    next(gb, None)
                    yield

            def g_s2(t):
                n = g * NTG + t
                K.dma(sp, Sblc[n % 2][:], Sbd[:, n, :], [Sbd], [Sblc[n % 2]])
                nla = nlas[t % 2]
                qd, kd, Am = qds[t % 2], kds[t % 2], Ams[t % 2]
                for d in range(2):
                    tri_t = tril if d == 0 else triu
                    pb = GB[1]
                    for h in range(4):
                        mm(pb[0:64, h * 128:(h + 1) * 128], nla[:, d * 256 + h * 64:d * 256 + (h + 1) * 64], tri_t[:, :], True, True, [nla, tri_t], [pb])
                    yield
                    K.op(act, lambda: nc.scalar.activation(Eq[d][:], pb[0:64, :], AF.Exp, scale=-1.0 / 16.0), [pb], [Eq[d]])
                    yield
                    K.op(act, lambda: nc.scalar.activation(Ek[d][:], pb[0:64, :], AF.Exp, scale=1.0 / 16.0), [pb], [Ek[d]])
                    yield
                    K.op(dve, lambda: nc.vector.scalar_tensor_tensor(
                        v3(qd[d][:]), qbT32[:, :, t * 128:(t + 1) * 128], 0.125, v3(Eq[d][:]), ALU.mult, ALU.mult), [qbT32, Eq[d]], [qd[d]])
                    yield
                    K.op(dve, lambda: nc.vector.tensor_tensor(
                        v3(kd[d][:]), kbT32[:, :, t * 128:(t + 1) * 128], v3(Ek[d][:]), ALU.mult), [kbT32, Ek[d]], [kd[d]])
                    if d == 0:
                        K.op(dve, lambda: nc.vector.tensor_copy(decs[t % 2][:], v3(Eq[0][:])[:, :, 127]), [Eq[0]], [decs[t % 2]])
                    yield
                    pa = GB[1]
                    for h in range(4):
                        mm(pa[:, h * 128:(h + 1) * 128], kd[d][:, h * 128:(h + 1) * 128], qd[d][:, h * 128:(h + 1) * 128], True, True, [kd[d], qd[d]], [pa])
                    yield
                    mk = mask_f if d == 0 else mask_b
                    K.op(dve, lambda: nc.vector.tensor_tensor(
                        v3(Am[d][:]), v3(pa[:, :]), mk[:].unsqueeze(1).broadcast_to([128, 4, 128]), ALU.mult), [pa, mk], [Am[d]])
                    yield
                yield from gla_tm_kd(t, 0, tril, nla, kdtms[t % 2], bank=GB[1])

            def g_s3(t):
                n = g * NTG + t
                qd, Am, Sbl, Sfb = qds[t % 2], Ams[t % 2], Sblc[n % 2], Sfbs[t % 2]
                state_enter(Sfb)
                yield
                po = GB[2]
                for h in range(4):
                    hs = slice(h * 128, (h + 1) * 128)
                    mm(po[:, hs], vbtm[:, t, hs], Am[0][:, hs], True, False, [vbtm, Am[0]], [po])
                    mm(po[:, hs], vbtm[:, t, hs], Am[1][:, hs], False, False, [vbtm, Am[1]], [po])
                    mm(po[:, hs], Sfb[0:64, hs], qd[0][:, hs], False, False, [Sfb, qd[0]], [po])
                    mm(po[:, hs], Sbl[0:64, hs], qd[1][:, hs], False, True, [Sbl, qd[1]], [po])
                K.op(act, lambda: nc.scalar.copy(obT32[:, :, t * 128:(t + 1) * 128], v3(po[:, :])), [po], [obT32])
                yield
                pu = gla_u0(t, kdtms[t % 2], bank=GB[2])
                yield
                state_leave(pu, decs[t % 2])
                yield
            interleave([attn_gen(), staged_gen(list(range(NTG)), [g_s1, g_s2, g_s3])])
            handoff(f32d_h, [f32d])
            sqs = [sqb, sqb2]
            K.op(act, lambda: nc.scalar.activation(sqs[0][:, 0:TG], obT32[:, 0, :], AF.Square), [obT32], [sqs[0]])
            for h in range(4):
                sq = sqs[h % 2]
                if h < 3:
                    sqn = sqs[(h + 1) % 2]
                    K.op(act, lambda h=h, sqn=sqn: nc.scalar.activation(sqn[:, 0:TG], obT32[:, h + 1, :], AF.Square), [obT32], [sqn])
                p2 = ps()
                mm(p2[:, 0:TG], ones_bf[:, :], sq[:, 0:TG], True, True, [ones_bf, sq], [p2])
                K.op(act, lambda p2=p2: nc.scalar.activation(f32c[:, 0:TG], p2[:, 0:TG], AF.Ln, bias=EPS, scale=1.0 / 128.0), [p2], [f32c])
                K.op(act, lambda: nc.scalar.activation(f32c[:, 0:TG], f32c[:, 0:TG], AF.Exp, scale=-0.5), [f32c], [f32c])
                K.op(dve, lambda h=h: nc.vector.scalar_tensor_tensor(f32b[:, 0:TG], obT32[:, h, :], gng[:], f32c[:, 0:TG], ALU.mult, ALU.mult), [obT32, gng, f32c], [f32b])
                K.op(dve, lambda h=h: nc.vector.tensor_tensor(obT[:, h, :], f32b[:, 0:TG], sog[:, h, :], ALU.mult), [f32b, sog], [obT])

            K.dma(sp, xg[:], xsrc[gsl(g), :].rearrange("(t p) c -> p t c", p=128), [xsrc], [xg])
            mark(f'L{l} g{g} merge')
            for c in range(8):
                cs = slice(c * 128, (c + 1) * 128)
                w = wnext()
                K.dma(sp, w[:], wt[l][T_MERGE + c][:, :, :], [wt[l][T_MERGE + c]], [w])
                wa_ = wbA[c % 2]
                K.dma(sp, wa_[:], wmad[l][c][:, :, :], [wmad[l][c]], [wa_])
                for j in range(3):
                    pg = proj_fm(w, j * 128, 128, hT, TG)
                    pp = ps()
                    if j == 0:
                        for h in range(4):
                            mm(pp[:, 0:TG], wa_[:, h, :], oaT[:, h, :], h == 0, h == 3, [wa_, oaT], [pp])
                    else:
                        brT = obT if j == 1 else omT
                        for h in range(4):
                            mm(pp[:, 0:TG], w[:, 4 * (j - 1) + h, 384:512], brT[:, h, :], h == 0, h == 3, [w, brT], [pp])
                    K.op(act, lambda pg=pg: nc.scalar.activation(f32a[:, 0:TG], pg[:, 0:TG], AF.Sigmoid), [pg], [f32a])
                    if j == 0:
                        K.op(dve, lambda pp=pp: nc.vector.tensor_tensor(f32d[:, 0:TG], pp[:, 0:TG], f32a[:, 0:TG], ALU.mult), [pp, f32a], [f32d])
                    else:
                        K.op(dve, lambda pp=pp: nc.vector.tensor_tensor(f32b[:, 0:TG], pp[:, 0:TG], f32a[:, 0:TG], ALU.mult), [pp, f32a], [f32b])
                        if j == 1:
                            K.op(dve, lambda: nc.vector.tensor_tensor(f32d[:, 0:TG], f32d[:, 0:TG], f32b[:, 0:TG], ALU.add), [f32d, f32b], [f32d])
                        else:
                            K.op(dve, lambda c=c: nc.vector.tensor_tensor(mergedT[:, c, :], f32d[:, 0:TG], f32b[:, 0:TG], ALU.add), [f32d, f32b], [mergedT])

            mark(f'L{l} g{g} outproj')
            handoff(vaug, [hbfs[2], hbfs[3]])
            wo = []
            for half in range(2):
                w = wnext()
                K.dma(sp, w[:], wt[l][T_OUT0 + half][:, :, :], [wt[l][T_OUT0 + half]], [w])
                wo.append(w)
            for t in range(NTG):
                for half in range(2):
                    p = ps()
                    for k in range(8):
                        mm(p[:, :], mergedT[:, k, t * 128:(t + 1) * 128], wo[half][:, k, :], k == 0, k == 7, [mergedT, wo[half]], [p], inc=(k == 7))
                    K.op(dve, lambda p=p, t=t, half=half: nc.vector.tensor_tensor(
                        xg[:, t, half * 512:(half + 1) * 512], p[:, :], xg[:, t, half * 512:(half + 1) * 512], ALU.add), [p, xg], [xg])
                norm_chain(xg, xg[:, t, :], gffn, t)
                if t >= 1:
                    norm_tr(t - 1, hT, (t - 1) * 128)
            mark(f'L{l} g{g} ffn')
            norm_tr(NTG - 1, hT, (NTG - 1) * 128)
            handoff(A1V, [actT])
            for fc in range(11):
                w = wnext()
                K.dma(sp, w[:], wt[l][T_FFI + fc][:, :, :], [wt[l][T_FFI + fc]], [w])
                for fi in range(2):
                    f = fc * 2 + fi
                    pg = proj_fm(w, fi * 128, 128, hT, TG)
                    pu = proj_fm(w, 256 + fi * 128, 128, hT, TG)
                    K.op(act, lambda pg=pg: nc.scalar.activation(f32a[:, 0:TG], pg[:, 0:TG], AF.Silu), [pg], [f32a])
                    K.op(dve, lambda pu=pu, f=f: nc.vector.tensor_tensor(actT[:, f, :], pu[:, 0:TG], f32a[:, 0:TG], ALU.mult), [pu, f32a], [actT])
            handoff([oaT, omT], [wfo[0]]); handoff([mergedT, obT], [wfo[1]])
            for half in range(2):
                pacc = [ps() for _ in range(NTG)]
                for kh in range(2):
                    wf = wfo[kh]
                    K.dma(sp, wf[:], wfod[l][half * 2 + kh][:, :, :], [wfod[l][half * 2 + kh]], [wf])
                    for t in range(NTG):
                        for f in range(11):
                            mm(pacc[t][:, :], actT[:, kh * 11 + f, t * 128:(t + 1) * 128], wf[:, f, :], kh == 0 and f == 0, kh == 1 and f == 10, [actT, wf], [pacc[t]], inc=(f == 10))
                for t in range(NTG):
                    pq = pacc[t]
                    K.op(dve, lambda t=t, half=half, pq=pq: nc.vector.tensor_tensor(
                        xg[:, t, half * 512:(half + 1) * 512], pq[:, :], xg[:, t, half * 512:(half + 1) * 512], ALU.add), [pq, xg], [xg])
            K.dma(pool, xdst[gsl(g), :].rearrange("(t p) c -> p t c", p=128), xg[:], [xg], [xdst])

    assert not jobs1 or depth == 1 or True
    mark('end')
    K.finish()
    es.close()
    return nc


def t5_bucket_np(rel):
    nb = 16
    max_exact = 8
    ret = (rel > 0).astype(np.int32) * nb
    n = np.abs(rel)
    nf = np.maximum(n, 1).astype(np.float32)
    large = max_exact + (np.log(nf / max_exact) / math.log(128 / max_exact) * (nb - max_exact)).astype(np.int32)
    large = np.minimum(large, nb - 1)
    return ret + np.where(n < max_exact, n, large)


def bias_layout(rel_bias):
    kk = np.arange(128)[:, None, None]
    j = np.arange(3)[None, :, None]
    q = np.arange(128)[None, None, :]
    rel = (j * 128 + kk) - 128 - q
    idx = t5_bucket_np(rel)
    gathered = np.asarray(rel_bias)[idx]
    gathered = gathered.reshape(128, 3, 128, 2, 4).transpose(0, 3, 1, 4, 2)
    return np.ascontiguousarray(gathered.reshape(128, 6, 512)).astype(np.float32)


_NAMES = ["norm_mix_g", "norm_ffn_g", "norm_mem_g", "w_in", "q_norm_a", "k_norm_a", "sink_a", "w_decay_up", "b_decay",
          "gla_norm_g", "w_mem_kv", "q_norm_m", "k_norm_m", "w_branch", "w_out", "w_ffn_in", "w_ffn_out"]


def kernel(**inputs):
    x = np.asarray(inputs["x"], dtype=np.float32)
    mem = np.asarray(inputs["mem"], dtype=np.float32)
    B, S, _ = x.shape
    nc = build(S)
    shared = {n: np.ascontiguousarray(np.asarray(inputs[n], dtype=np.float32)) for n in _NAMES}
    shared["biasg"] = bias_layout(inputs["rel_bias"])
    in_maps = []
    for b in range(B):
        m = dict(shared)
        m["x"] = np.ascontiguousarray(x[b])
        m["mem"] = np.ascontiguousarray(mem[b])
        in_maps.append(m)
    res = run_bass_kernel_spmd(nc, in_maps, core_ids=list(range(B)))
    out = np.stack([np.asarray(r["y"], dtype=np.float32) for r in res.results], axis=0)
    return out
```
